# Optimizing a Trainium2 kernel written in Bass

```python
import jax
import jax.numpy as jnp
from jax import lax
import numpy as np

D_MODEL = 1024
BATCH = 1
SEQ = 16384
DEPTH = 4

GRID_W = 64
CTX_LEN = 256
HEAD_DIM = 64
BRANCH_WIDTH = 256
N_BRANCH = 4
A_HEADS = 4
A_KV_HEADS = 2
A_GROUP = A_HEADS // A_KV_HEADS
Q_BLOCK = 128
ROPE_THETA = 10000.0
B_HEADS = 4
B_KEY_DIM = 64
B_VAL_DIM = 64
B_CHUNK = 64
C_GROUPS = 4
C_GROUP_DIM = BRANCH_WIDTH // C_GROUPS
C_WINDOWS = (2, 4, 8, 16)
D_HEADS = 4
NA_WIN_R = 8
NA_WIN_C = 16
D_FF = 2816
N_MOD = 9
EPS = 1e-6
ATTN_SCALE = HEAD_DIM ** -0.5

IN_SPLITS = (
    A_HEADS * HEAD_DIM, A_KV_HEADS * HEAD_DIM, A_KV_HEADS * HEAD_DIM,
    B_HEADS * B_KEY_DIM, B_HEADS * B_KEY_DIM, B_HEADS * B_KEY_DIM, B_HEADS * B_VAL_DIM, B_HEADS * B_VAL_DIM,
    BRANCH_WIDTH,
    D_HEADS * HEAD_DIM, D_HEADS * HEAD_DIM, D_HEADS * HEAD_DIM,
    N_BRANCH * D_MODEL,
)
IN_WIDTH = sum(IN_SPLITS)

kernel_name = 'hybrid_prefix_dit_block'


def rmsnorm(x, g):
    xf = x.astype(jnp.float32)
    y = xf * lax.rsqrt(jnp.mean(xf * xf, axis=-1, keepdims=True) + EPS)
    return (y * g.astype(jnp.float32)).astype(x.dtype)


def modulate(x, shift, scale):
    return x * (1 + scale) + shift


def swiglu(z, wg, wu, wd):
    return (jax.nn.silu(z @ wg) * (z @ wu)) @ wd


def ffn_half(z, shift, scale, gate, norm, wg, wu, wd):
    return z + 0.5 * gate * swiglu(modulate(rmsnorm(z, norm), shift, scale), wg, wu, wd)


def rope_tables(n):
    t = jnp.arange(n, dtype=jnp.int32)
    half = HEAD_DIM // 2
    inv = 1.0 / (ROPE_THETA ** (jnp.arange(0, half, 2, dtype=jnp.float32) / half))
    ang_r = (t // GRID_W).astype(jnp.float32)[:, None] * inv
    ang_c = (t % GRID_W).astype(jnp.float32)[:, None] * inv
    tab = jnp.stack([jnp.cos(ang_r), jnp.sin(ang_r), jnp.cos(ang_c), jnp.sin(ang_c)])
    return tab[:, :, None, :]


def _rotate(x, cos, sin):
    x1, x2 = jnp.split(x, 2, axis=-1)
    return jnp.concatenate([x1 * cos - x2 * sin, x2 * cos + x1 * sin], axis=-1)


def apply_axial_rope(x, rope):
    half = HEAD_DIM // 2
    xf = x.astype(jnp.float32)
    out = jnp.concatenate([_rotate(xf[..., :half], rope[0], rope[1]),
                           _rotate(xf[..., half:], rope[2], rope[3])], axis=-1)
    return out.astype(x.dtype)


def dense_attend(q, k, v):
    s = jnp.einsum('bqhgd,bkhd->bhgqk', q, k).astype(jnp.float32) * ATTN_SCALE
    p = jax.nn.softmax(s, axis=-1).astype(v.dtype)
    return jnp.einsum('bhgqk,bkhd->bqhgd', p, v)


def gqa_mixer(q, k, v, qc, kc, vc, q_gain, k_gain, rope, with_ctx_out):
    b, t = q.shape[:2]

    def norm_heads(a, n, g):
        return rmsnorm(a.reshape(a.shape[0], a.shape[1], n, HEAD_DIM), g)

    ql = apply_axial_rope(norm_heads(q, A_HEADS, q_gain), rope)
    kl = apply_axial_rope(norm_heads(k, A_KV_HEADS, k_gain), rope)
    vl = v.reshape(b, t, A_KV_HEADS, HEAD_DIM)
    kcx = norm_heads(kc, A_KV_HEADS, k_gain)
    vcx = vc.reshape(b, vc.shape[1], A_KV_HEADS, HEAD_DIM)
    k_all = jnp.concatenate([kcx, kl], axis=1)
    v_all = jnp.concatenate([vcx, vl], axis=1)
    qb = ql.reshape(b, t // Q_BLOCK, Q_BLOCK, A_KV_HEADS, A_GROUP, HEAD_DIM)
    ob = lax.map(lambda blk: dense_attend(blk, k_all, v_all), jnp.moveaxis(qb, 1, 0))
    y = jnp.moveaxis(ob, 0, 1).reshape(b, t, BRANCH_WIDTH)
    if not with_ctx_out:
        return y, None
    qcx = norm_heads(qc, A_HEADS, q_gain).reshape(b, qc.shape[1], A_KV_HEADS, A_GROUP, HEAD_DIM)
    yc = dense_attend(qcx, kcx, vcx).reshape(b, qc.shape[1], BRANCH_WIDTH)
    return y, yc


def gla_chunk_scan(q, k, v, log_f, s0):
    b, t, h, _ = q.shape
    n = t // B_CHUNK

    def to_chunks(a):
        return a.reshape(b, n, B_CHUNK, h, a.shape[-1]).transpose(1, 0, 3, 2, 4)

    causal = jnp.tril(jnp.ones((B_CHUNK, B_CHUNK), dtype=bool))[:, :, None]

    def step(s, inp):
        qc, kc, vc, lc = inp
        cum = jnp.cumsum(lc, axis=2)
        diff = cum[:, :, :, None, :] - cum[:, :, None, :, :]
        decay = jnp.exp(jnp.where(causal, diff, -jnp.inf))
        att = jnp.einsum('bhtd,bhsd,bhtsd->bhts', qc, kc, decay)
        o = jnp.einsum('bhts,bhsv->bhtv', att, vc) + jnp.einsum('bhtd,bhdv->bhtv', qc * jnp.exp(cum), s)
        last = cum[:, :, -1:, :]
        s_new = jnp.exp(last[:, :, 0, :])[..., None] * s + jnp.einsum('bhsd,bhsv->bhdv', kc * jnp.exp(last - cum), vc)
        return s_new, o

    s_fin, o = lax.scan(step, s0, (to_chunks(q), to_chunks(k), to_chunks(v), to_chunks(log_f)))
    return o.transpose(1, 0, 3, 2, 4).reshape(b, t, h, v.shape[-1]), s_fin


def hgrn2_mixer(lat, cxt, lb, o_gain, with_ctx_out):
    def heads(a):
        return a.reshape(a.shape[0], a.shape[1], B_HEADS, -1).astype(jnp.float32)

    def gates(f_pre, lbd):
        f = lbd + (1.0 - lbd) * jax.nn.sigmoid(heads(f_pre))
        return 1.0 - f, jnp.log(f)

    ql, ffl, fbl, il, gl = lat
    qc, ffc, fbc, ic, gc = cxt
    qscale = B_KEY_DIM ** -0.5
    qlh, ilh = heads(ql) * qscale, heads(il)
    qch, ich = heads(qc) * qscale, heads(ic)
    s0 = jnp.zeros((ql.shape[0], B_HEADS, B_KEY_DIM, B_VAL_DIM), jnp.float32)
    o_lat, o_ctx = None, None
    for f_l, f_c, lbd, reverse in ((ffl, ffc, lb[0], False), (fbl, fbc, lb[1], True)):
        rev = (lambda a: jnp.flip(a, axis=1)) if reverse else (lambda a: a)
        kl, lfl = gates(f_l, lbd)
        kc, lfc = gates(f_c, lbd)
        oc, s_ctx = gla_chunk_scan(rev(qch), rev(kc), rev(ich), rev(lfc), s0)
        ol, _ = gla_chunk_scan(rev(qlh), rev(kl), rev(ilh), rev(lfl), s_ctx)
        o_lat = rev(ol) if o_lat is None else o_lat + rev(ol)
        o_ctx = rev(oc) if o_ctx is None else o_ctx + rev(oc)

    def readout(o, g):
        y = rmsnorm(o, o_gain) * jax.nn.silu(heads(g))
        return y.reshape(g.shape[0], g.shape[1], BRANCH_WIDTH).astype(g.dtype)

    y = readout(o_lat, gl)
    if not with_ctx_out:
        return y, None
    return y, readout(o_ctx, gc)


def window_mean(x, w):
    t = x.shape[1]
    xf = x.astype(jnp.float32)
    cs = jnp.concatenate([jnp.zeros_like(xf[:, :1]), lax.cumsum(xf, axis=1)], axis=1)
    pos = jnp.arange(t)
    lo = jnp.clip(pos - w // 2, 0, t)
    hi = jnp.clip(pos - w // 2 + w, 0, t)
    mean = (cs[:, hi] - cs[:, lo]) / (hi - lo).astype(jnp.float32)[None, :, None]
    return mean.astype(x.dtype)


def pool_mixer(xp, w_group, scale):
    b, t, _ = xp.shape
    xg = xp.reshape(b, t, C_GROUPS, C_GROUP_DIM)
    pooled = jnp.stack([window_mean(xg[:, :, g], w) - xg[:, :, g] for g, w in enumerate(C_WINDOWS)], axis=2)
    y = jnp.einsum('btgc,gcd->btgd', pooled, w_group).reshape(b, t, BRANCH_WIDTH)
    return y * scale


def na_mixer(q, k, v, qc, kc, vc, rel_bias, with_ctx_out):
    b, t = q.shape[:2]
    rows = t // GRID_W
    wr = min(NA_WIN_R, rows)
    wc = NA_WIN_C

    def grid(a):
        return a.reshape(b, rows, GRID_W, D_HEADS, HEAD_DIM)

    qg, kg, vg = grid(q), grid(k), grid(v)
    n_ctx = kc.shape[1]
    kcx = kc.reshape(b, n_ctx, D_HEADS, HEAD_DIM)
    vcx = vc.reshape(b, n_ctx, D_HEADS, HEAD_DIM)
    col = jnp.arange(GRID_W)
    col_idx = jnp.clip(col - wc // 2, 0, GRID_W - wc)[:, None] + jnp.arange(wc)
    dc_idx = col_idx - col[:, None] + (NA_WIN_C - 1)

    def row_block(r):
        rs = jnp.clip(r - wr // 2, 0, rows - wr)
        k_win = lax.dynamic_slice_in_dim(kg, rs, wr, axis=1)[:, :, col_idx]
        v_win = lax.dynamic_slice_in_dim(vg, rs, wr, axis=1)[:, :, col_idx]
        q_row = lax.dynamic_index_in_dim(qg, r, axis=1, keepdims=False)
        dr_idx = rs + jnp.arange(wr) - r + (NA_WIN_R - 1)
        bias = rel_bias[:, dr_idx][:, :, dc_idx]
        bias = jnp.transpose(bias, (0, 2, 1, 3)).reshape(D_HEADS, GRID_W, wr * wc).astype(jnp.float32)
        s_loc = jnp.einsum('bqhd,bwqjhd->bhqwj', q_row, k_win).reshape(b, D_HEADS, GRID_W, wr * wc)
        s_loc = s_loc.astype(jnp.float32) * ATTN_SCALE + bias
        s_ctx = jnp.einsum('bqhd,bkhd->bhqk', q_row, kcx).astype(jnp.float32) * ATTN_SCALE
        p = jax.nn.softmax(jnp.concatenate([s_ctx, s_loc], axis=-1), axis=-1).astype(v.dtype)
        p_ctx = p[..., :n_ctx]
        p_loc = p[..., n_ctx:].reshape(b, D_HEADS, GRID_W, wr, wc)
        return jnp.einsum('bhqk,bkhd->bqhd', p_ctx, vcx) + jnp.einsum('bhqwj,bwqjhd->bqhd', p_loc, v_win)

    o = lax.map(row_block, jnp.arange(rows))
    y = jnp.moveaxis(o, 0, 1).reshape(b, t, BRANCH_WIDTH)
    if not with_ctx_out:
        return y, None
    qcx = qc.reshape(b, n_ctx, D_HEADS, 1, HEAD_DIM)
    yc = dense_attend(qcx, kcx, vcx).reshape(b, n_ctx, BRANCH_WIDTH)
    return y, yc


def merge_branches(branches, gate_pre, w_branch, w_out):
    gates = jax.nn.sigmoid(gate_pre).reshape(gate_pre.shape[0], gate_pre.shape[1], N_BRANCH, D_MODEL)
    merged = gates[..., 0, :] * (branches[0] @ w_branch[0])
    for n in range(1, N_BRANCH):
        merged = merged + gates[..., n, :] * (branches[n] @ w_branch[n])
    return merged @ w_out


def token_mixing(xn, cn, w_in, a_q_norm, a_k_norm, lb, b_o_norm, c_w_group, c_scale, d_rel_bias,
                 w_branch, w_out, rope, with_ctx_out):
    offs = np.cumsum(IN_SPLITS)[:-1].tolist()
    pl = jnp.split(xn @ w_in, offs, axis=-1)
    pc = jnp.split(cn @ w_in, offs, axis=-1)
    ya, ya_c = gqa_mixer(pl[0], pl[1], pl[2], pc[0], pc[1], pc[2], a_q_norm, a_k_norm, rope, with_ctx_out)
    yb, yb_c = hgrn2_mixer(pl[3:8], pc[3:8], lb, b_o_norm, with_ctx_out)
    yp = pool_mixer(pl[8], c_w_group, c_scale)
    yd, yd_c = na_mixer(pl[9], pl[10], pl[11], pc[9], pc[10], pc[11], d_rel_bias, with_ctx_out)
    y_lat = merge_branches((ya, yb, yp, yd), pl[12], w_branch, w_out)
    if not with_ctx_out:
        return y_lat, None
    yp_c = pool_mixer(pc[8], c_w_group, c_scale)
    y_ctx = merge_branches((ya_c, yb_c, yp_c, yd_c), pc[12], w_branch, w_out)
    return y_lat, y_ctx


def setup_inputs(seed: int = 0) -> dict:
    key = jax.random.key(seed)
    ks = jax.random.split(key, 26)
    L, D = DEPTH, D_MODEL

    def w(k, shape, fan_in, gain=1.0):
        return jax.random.normal(k, shape, jnp.float32) * (gain * fan_in ** -0.5)

    def g(k, shape):
        return 1.0 + 0.02 * jax.random.normal(k, shape, jnp.float32)

    def nrm(k, shape, s):
        return s * jax.random.normal(k, shape, jnp.float32)

    return {
        'x': nrm(ks[0], (BATCH, SEQ, D), 1.0),
        'c': nrm(ks[1], (BATCH, D), 1.0),
        'ctx': nrm(ks[2], (BATCH, CTX_LEN, D), 1.0),
        'c_ctx': nrm(ks[3], (D,), 1.0),
        'w_ada': w(ks[4], (L, D, N_MOD * D), D, 0.5),
        'b_ada': nrm(ks[5], (L, N_MOD * D), 0.01),
        'ffn1_norm': g(ks[6], (L, D)),
        'ffn1_w_gate': w(ks[7], (L, D, D_FF), D),
        'ffn1_w_up': w(ks[8], (L, D, D_FF), D),
        'ffn1_w_down': w(ks[9], (L, D_FF, D), D_FF),
        'mix_norm': g(ks[10], (L, D)),
        'w_in': w(ks[11], (L, D, IN_WIDTH), D),
        'a_q_norm': g(ks[12], (L, HEAD_DIM)),
        'a_k_norm': g(ks[13], (L, HEAD_DIM)),
        'b_lb_logits': nrm(ks[14], (L, 2, B_HEADS * B_KEY_DIM), 0.5),
        'b_o_norm': g(ks[15], (L, B_VAL_DIM)),
        'c_w_group': w(ks[16], (L, C_GROUPS, C_GROUP_DIM, C_GROUP_DIM), C_GROUP_DIM),
        'c_scale': g(ks[17], (L, BRANCH_WIDTH)),
        'd_rel_bias': nrm(ks[18], (L, D_HEADS, 2 * NA_WIN_R - 1, 2 * NA_WIN_C - 1), 0.1),
        'w_branch': w(ks[19], (L, N_BRANCH, BRANCH_WIDTH, D), BRANCH_WIDTH),
        'w_out': w(ks[20], (L, D, D), D),
        'ffn2_norm': g(ks[21], (L, D)),
        'ffn2_w_gate': w(ks[22], (L, D, D_FF), D),
        'ffn2_w_up': w(ks[23], (L, D, D_FF), D),
        'ffn2_w_down': w(ks[24], (L, D_FF, D), D_FF),
        'final_norm': g(ks[25], (D,)),
    }


def reference(x, c, ctx, c_ctx, w_ada, b_ada, ffn1_norm, ffn1_w_gate, ffn1_w_up, ffn1_w_down,
              mix_norm, w_in, a_q_norm, a_k_norm, b_lb_logits, b_o_norm, c_w_group, c_scale,
              d_rel_bias, w_branch, w_out, ffn2_norm, ffn2_w_gate, ffn2_w_up, ffn2_w_down, final_norm):
    rope = rope_tables(x.shape[1])
    lb_all = jnp.cumsum(jax.nn.softmax(b_lb_logits.astype(jnp.float32), axis=0), axis=0)
    lb_all = (lb_all - lb_all[:1]).reshape(DEPTH, 2, B_HEADS, B_KEY_DIM)
    h = ctx
    for l in range(DEPTH):
        with_ctx_out = l < DEPTH - 1
        mod = [m[:, None, :] for m in jnp.split(jax.nn.silu(c) @ w_ada[l] + b_ada[l], N_MOD, axis=-1)]
        mod_c = jnp.split(jax.nn.silu(c_ctx) @ w_ada[l] + b_ada[l], N_MOD, axis=-1)
        x = ffn_half(x, mod[0], mod[1], mod[2], ffn1_norm[l], ffn1_w_gate[l], ffn1_w_up[l], ffn1_w_down[l])
        h = ffn_half(h, mod_c[0], mod_c[1], mod_c[2], ffn1_norm[l], ffn1_w_gate[l], ffn1_w_up[l], ffn1_w_down[l])
        xn = modulate(rmsnorm(x, mix_norm[l]), mod[3], mod[4])
        hn = modulate(rmsnorm(h, mix_norm[l]), mod_c[3], mod_c[4])
        y, y_c = token_mixing(xn, hn, w_in[l], a_q_norm[l], a_k_norm[l], lb_all[l], b_o_norm[l],
                              c_w_group[l], c_scale[l], d_rel_bias[l], w_branch[l], w_out[l], rope, with_ctx_out)
        x = x + mod[5] * y
        x = ffn_half(x, mod[6], mod[7], mod[8], ffn2_norm[l], ffn2_w_gate[l], ffn2_w_up[l], ffn2_w_down[l])
        if with_ctx_out:
            h = h + mod_c[5] * y_c
            h = ffn_half(h, mod_c[6], mod_c[7], mod_c[8], ffn2_norm[l], ffn2_w_gate[l], ffn2_w_up[l], ffn2_w_down[l])
    return rmsnorm(x, final_norm)
```

```python
from contextlib import ExitStack

import ml_dtypes
import numpy as np

import concourse.bass as bass
import concourse.mybir as mybir
from concourse.bass_utils import run_bass_kernel_spmd

F32 = mybir.dt.float32
BF16 = mybir.dt.bfloat16
ALU = mybir.AluOpType
AF = mybir.ActivationFunctionType

NCORES = 8
D = 1024
KC = 8
DFF = 2816
JC = 22
SEQ = 16384
NLAT = SEQ // NCORES
NCTX = 256
NTOK = NLAT + NCTX
INW = 6912
EPS = 1e-6
SEGS = [(0, 512, 0), (512, 512, 0), (1024, 512, 0), (1536, 512, 0), (2048, 256, 1)]
FFN_PASSES = [[0, 1], [2, 3, 4]]


class Tr:
    __slots__ = ("name", "w", "r", "dsem")

    def __init__(self, name):
        self.name = name
        self.w = {}
        self.r = {}
        self.dsem = None


class Sched:
    ENG = ("pe", "act", "dve", "pool", "sp")

    def __init__(self, nc, es):
        self.nc = nc
        self.es = es
        self.ops = {e: [] for e in self.ENG}
        self.cnt = {}
        self.sems = {}
        self.waited = {e: {} for e in self.ENG}
        for e in ("pe", "act", "dve", "pool"):
            self.sems[e] = es.enter_context(nc.semaphore("s_" + e))
            self.cnt[e] = 0
        self.ndsem = 0

    def _deps(self, eng, reads, writes, nowaw):
        deps = {}

        def add(d):
            for k, v in d.items():
                if deps.get(k, 0) < v:
                    deps[k] = v

        for t in reads:
            add(t.w)
        for t in writes:
            add(t.r)
            if not nowaw:
                add(t.w)
        waits = []
        wd = self.waited[eng]
        for k, v in deps.items():
            if wd.get(k, 0) < v:
                wd[k] = v
                waits.append((self.sems[k], v))
        return waits

    def _commit(self, key, val, reads, writes, nowaw):
        for t in reads:
            if t.r.get(key, 0) < val:
                t.r[key] = val
        for t in writes:
            if nowaw:
                if t.w.get(key, 0) < val:
                    t.w[key] = val
            else:
                t.w = {key: val}
                t.r = {}

    def op(self, eng, fn, reads=(), writes=(), nowaw=False):
        waits = self._deps(eng, reads, writes, nowaw)
        self.cnt[eng] += 1
        self.ops[eng].append((waits, fn, self.sems[eng], 1))
        self._commit(eng, self.cnt[eng], reads, writes, nowaw)

    def dma(self, eng, out_ap, in_ap, reads, write, nowaw=False):
        if write.dsem is None:
            self.ndsem += 1
            key = "d%d" % self.ndsem
            self.sems[key] = self.es.enter_context(self.nc.semaphore(key))
            self.cnt[key] = 0
            write.dsem = key
        key = write.dsem
        waits = self._deps(eng, reads, (write,), nowaw)
        self.cnt[key] += 16
        self.ops[eng].append((waits, lambda e: e.dma_start(out=out_ap, in_=in_ap), self.sems[key], 16))
        self._commit(key, self.cnt[key], reads, (write,), nowaw)

    def barrier(self):
        for eng in self.ENG:
            waits = []
            wd = self.waited[eng]
            for k, v in self.cnt.items():
                if v > 0 and wd.get(k, 0) < v:
                    wd[k] = v
                    waits.append((self.sems[k], v))
            self.ops[eng].append((waits, None, None, 0))

    def finish(self, eng, outs):
        waits = self._deps(eng, outs, (), False)
        self.ops[eng].append((waits, None, None, 0))

    def emit(self):
        nc = self.nc

        def replay(name):
            def f(e):
                for waits, fn, sem, inc in self.ops[name]:
                    for s, v in waits:
                        e.wait_ge(s, v)
                    if fn is not None:
                        ins = fn(e)
                        ins.then_inc(sem, inc)
            return f

        with nc.Block() as block:
            block.tensor(replay("pe"))
            block.scalar(replay("act"))
            block.vector(replay("dve"))
            block.gpsimd(replay("pool"))
            block.sync(replay("sp"))


class Ctx:
    def __init__(self, nc, es):
        self.nc = nc
        self.es = es
        self.S = Sched(nc, es)
        self.psum = []
        for i in range(8):
            t = es.enter_context(nc.psum_tensor("ps%d" % i, [128, 512], F32))
            self.psum.append((t, Tr("ps%d" % i)))
        self.pi = 0

    def ps(self):
        r = self.psum[self.pi % 8]
        self.pi += 1
        return r

    def sb(self, name, shape, dt, es=None):
        self.uid = getattr(self, "uid", 0) + 1
        name = "sb%d_%s" % (self.uid, name)
        t = (es or self.es).enter_context(self.nc.sbuf_tensor(name, shape, dt))
        return t, Tr(name)

    def dram_in(self, name, shape, dt=F32):
        return self.nc.dram_tensor(name, list(shape), dt, kind="ExternalInput").ap()

    def dram_out(self, name, shape, dt=F32):
        return self.nc.dram_tensor(name, list(shape), dt, kind="ExternalOutput").ap()


def mm_group(C, ps, ps_tr, out_ap, pairs, reads):
    n = len(pairs)

    def fn(e):
        ins = None
        for i, (l, r) in enumerate(pairs):
            ins = e.matmul(out_ap, l, r, start=(i == 0), stop=(i == n - 1))
        return ins

    C.S.op("pe", fn, reads=reads, writes=(ps_tr,), nowaw=False)


def emit_mod(C, wada_d, bada_d, cc_d, norms_d, l, modv, modv_tr):
    S = C.S
    with ExitStack() as es:
        cc, cc_tr = C.sb("cc", [128, KC, 2], F32, es)
        sc, sc_tr = C.sb("sc", [128, KC, 2], BF16, es)
        bada, bada_tr = C.sb("bada", [128, 72], F32, es)
        nrm, nrm_tr = C.sb("nrm", [128, 3, KC], F32, es)
        modr, modr_tr = C.sb("modr", [128, 72, 2], F32, es)
        wa = [C.sb("wada%d" % i, [128, KC, 512], BF16, es) for i in range(2)]
        S.dma("sp", cc[:], cc_d, (), cc_tr)
        S.dma("sp", bada[:], bada_d[l], (), bada_tr)
        S.dma("sp", nrm[:], norms_d[l], (), nrm_tr)
        S.op("act", lambda e: e.activation(sc[:], cc[:], AF.Silu), reads=(cc_tr,), writes=(sc_tr,))
        ps, ps_tr = C.ps()
        wv = wada_d[l].rearrange("(k p) n -> p k n", p=128)
        for piece in range(18):
            wt, wt_tr = wa[piece % 2]
            S.dma("pool", wt[:], wv[:, :, piece * 512:(piece + 1) * 512], (), wt_tr)
            for jj in range(4):
                j = piece * 4 + jj
                pairs = [(wt[:, k, jj * 128:(jj + 1) * 128], sc[:, k, :]) for k in range(KC)]
                mm_group(C, ps, ps_tr, ps[:, 2 * j:2 * j + 2], pairs, (wt_tr, sc_tr))
        pv = ps[:, 0:144].rearrange("p (j o) -> p j o", o=2)
        S.op("dve", lambda e: e.tensor_tensor(modr[:], pv, bada[:].unsqueeze(2).to_broadcast([128, 72, 2]), ALU.add),
             reads=(ps_tr, bada_tr), writes=(modr_tr,))
        for f in range(3):
            sh = modr[:, (3 * f) * 8:(3 * f + 1) * 8, :]
            scl = modr[:, (3 * f + 1) * 8:(3 * f + 2) * 8, :]
            gt = modr[:, (3 * f + 2) * 8:(3 * f + 3) * 8, :]
            if f == 1:
                pass
            nb = nrm[:, f, :].unsqueeze(2).to_broadcast([128, KC, 2])
            S.op("dve", lambda e, scl=scl, nb=nb, f=f: e.scalar_tensor_tensor(
                modv[:, 3 * f, :, :], scl, 1.0, nb, ALU.add, ALU.mult),
                reads=(modr_tr, nrm_tr), writes=(modv_tr,), nowaw=True)
            S.op("dve", lambda e, sh=sh, f=f: e.tensor_copy(modv[:, 3 * f + 1, :, :], sh),
                 reads=(modr_tr,), writes=(modv_tr,), nowaw=True)
            gsc = 1.0 if f == 1 else 0.5
            S.op("dve", lambda e, gt=gt, f=f, gsc=gsc: e.tensor_scalar_mul(modv[:, 3 * f + 2, :, :], gt, gsc),
                 reads=(modr_tr,), writes=(modv_tr,), nowaw=True)
        S.barrier()


def emit_norm(C, W, x, x_tr, xn, xn_tr, xn_off, seg, modv, modv_tr, kind):
    S = C.S
    si, (s0, n, o) = seg
    sq, sq_tr = W["sq"]
    rs, rs_tr = W["rs"]
    tmp, tmp_tr = W["tmp"]
    ones = W["ones"]
    for k in range(KC):
        S.op("dve", lambda e, k=k: e.tensor_tensor(sq[:, k, 0:n], x[:, k, s0:s0 + n], x[:, k, s0:s0 + n], ALU.mult),
             reads=(x_tr[si],), writes=(sq_tr,), nowaw=(k > 0))
    ps, ps_tr = C.ps()
    mm_group(C, ps, ps_tr, ps[:, 0:n], [(ones[0][:, :], sq[:, k, 0:n]) for k in range(KC)], (sq_tr, ones[1]))
    S.op("dve", lambda e: e.tensor_scalar(rs[:, 0:n], ps[:, 0:n], 1.0 / D, EPS, ALU.mult, ALU.add),
         reads=(ps_tr,), writes=(rs_tr,))
    S.op("act", lambda e: e.activation(rs[:, 0:n], rs[:, 0:n], AF.Sqrt), reads=(rs_tr,), writes=(rs_tr,))
    S.op("dve", lambda e: e.reciprocal(rs[:, 0:n], rs[:, 0:n]), reads=(rs_tr,), writes=(rs_tr,))
    for k in range(KC):
        t, t_tr = tmp[k % 2], tmp_tr[k % 2]
        S.op("dve", lambda e, k=k, t=t: e.tensor_tensor(t[:, 0:n], x[:, k, s0:s0 + n], rs[:, 0:n], ALU.mult),
             reads=(x_tr[si], rs_tr), writes=(t_tr,))
        S.op("act", lambda e, k=k, t=t: e.activation(xn[:, k, xn_off:xn_off + n], t[:, 0:n], AF.Identity,
                                                     bias=modv[:, kind + 1, k, o:o + 1],
                                                     scale=modv[:, kind, k, o:o + 1]),
             reads=(t_tr, modv_tr), writes=(xn_tr,), nowaw=(k > 0))


def emit_ffn(C, W, x, x_tr, modv, modv_tr, kind, wg_d, wu_d, wd_d, l):
    S = C.S
    wgv = wg_d[l].rearrange("(k p) n -> p k n", p=128)
    wuv = wu_d[l].rearrange("(k p) n -> p k n", p=128)
    wdv = wd_d[l].rearrange("(j p) c -> p j c", p=128)
    with ExitStack() as es:
        PT = max(sum(SEGS[si][1] for si in p) for p in FFN_PASSES)
        xn, _ = C.sb("f_xn", [128, KC, PT], BF16, es)
        h, _ = C.sb("f_h", [128, JC, PT], BF16, es)
        wg = [C.sb("f_wg%d" % i, [128, KC, 256], BF16, es) for i in range(2)]
        wu = [C.sb("f_wu%d" % i, [128, KC, 256], BF16, es) for i in range(2)]
        wd = [C.sb("f_wd%d" % i, [128, JC, 256], BF16, es) for i in range(2)]
        sg = [C.sb("f_sg%d" % i, [128, 512], F32, es) for i in range(2)]
        sgi = 0
        for p in FFN_PASSES:
            offs = {}
            o_ = 0
            xn_tr = {}
            h_tr = {}
            for si in p:
                offs[si] = o_
                o_ += SEGS[si][1]
                xn_tr[si] = Tr("xn%d" % si)
                h_tr[si] = Tr("h%d" % si)
                emit_norm(C, W, x, x_tr, xn, xn_tr[si], offs[si], (si, SEGS[si]), modv, modv_tr, kind)
            for jb in range(JC // 2):
                g_t, g_tr = wg[jb % 2]
                u_t, u_tr = wu[jb % 2]
                S.dma("pool", g_t[:], wgv[:, :, jb * 256:(jb + 1) * 256], (), g_tr)
                S.dma("pool", u_t[:], wuv[:, :, jb * 256:(jb + 1) * 256], (), u_tr)
                for si in p:
                    s0, n, o = SEGS[si]
                    xo = offs[si]
                    for jj in range(2):
                        j = jb * 2 + jj
                        pg, pg_tr = C.ps()
                        pu, pu_tr = C.ps()
                        mm_group(C, pg, pg_tr, pg[:, 0:n],
                                 [(g_t[:, k, jj * 128:(jj + 1) * 128], xn[:, k, xo:xo + n]) for k in range(KC)],
                                 (g_tr, xn_tr[si]))
                        mm_group(C, pu, pu_tr, pu[:, 0:n],
                                 [(u_t[:, k, jj * 128:(jj + 1) * 128], xn[:, k, xo:xo + n]) for k in range(KC)],
                                 (u_tr, xn_tr[si]))
                        st, st_tr = sg[sgi % 2]
                        sgi += 1
                        S.op("act", lambda e, st=st, pg=pg, n=n: e.activation(st[:, 0:n], pg[:, 0:n], AF.Silu),
                             reads=(pg_tr,), writes=(st_tr,))
                        S.op("dve", lambda e, st=st, pu=pu, n=n, j=j, xo=xo: e.tensor_tensor(
                            h[:, j, xo:xo + n], st[:, 0:n], pu[:, 0:n], ALU.mult),
                            reads=(st_tr, pu_tr), writes=(h_tr[si],), nowaw=(j > 0))
            for cb in range(4):
                d_t, d_tr = wd[cb % 2]
                S.dma("pool", d_t[:], wdv[:, :, cb * 256:(cb + 1) * 256], (), d_tr)
                for si in p:
                    s0, n, o = SEGS[si]
                    xo = offs[si]
                    for cc_ in range(2):
                        c = cb * 2 + cc_
                        py, py_tr = C.ps()
                        mm_group(C, py, py_tr, py[:, 0:n],
                                 [(d_t[:, j, cc_ * 128:(cc_ + 1) * 128], h[:, j, xo:xo + n]) for j in range(JC)],
                                 (d_tr, h_tr[si]))
                        S.op("dve", lambda e, py=py, n=n, c=c, s0=s0, o=o: e.scalar_tensor_tensor(
                            x[:, c, s0:s0 + n], py[:, 0:n], modv[:, kind + 2, c, o:o + 1], x[:, c, s0:s0 + n],
                            ALU.mult, ALU.add),
                            reads=(py_tr, modv_tr, x_tr[si]), writes=(x_tr[si],), nowaw=False)
        S.barrier()


def alloc_work(C, es):
    W = {}
    W["sq"] = C.sb("w_sq", [128, KC, 512], BF16, es)
    W["rs"] = C.sb("w_rs", [128, 512], F32, es)
    t0 = C.sb("w_tmp0", [128, 512], F32, es)
    t1 = C.sb("w_tmp1", [128, 512], F32, es)
    W["tmp"] = ([t0[0], t1[0]], [t0[1], t1[1]])
    ones = C.sb("w_ones", [128, 128], BF16, es)
    C.S.op("dve", lambda e: e.memset(ones[0][:], 1.0), writes=(ones[1],))
    W["ones"] = ones
    return W


def build_ffn_test():
    nc = bass.Bass("TRN2", target_bir_lowering=False)
    with ExitStack() as es:
        C = Ctx(nc, es)
        S = C.S
        xT_d = C.dram_in("xT", [D, NTOK])
        cc_d = C.dram_in("cc", [128, KC, 2])
        wada_d = C.dram_in("w_ada", [1, D, 9 * D])
        bada_d = C.dram_in("b_ada", [1, 128, 72])
        norms_d = C.dram_in("norms", [1, 128, 3, KC])
        wg_d = C.dram_in("ffn1_wg", [1, D, DFF])
        wu_d = C.dram_in("ffn1_wu", [1, D, DFF])
        wd_d = C.dram_in("ffn1_wd", [1, DFF, D])
        out_d = C.dram_out("xo", [D, NTOK])
        modo_d = C.dram_out("modo", [128, 9 * KC * 2])
        x, _ = C.sb("x", [128, KC, NTOK], F32)
        x_tr = [Tr("x%d" % i) for i in range(len(SEGS))]
        modv, modv_tr = C.sb("modv", [128, 9, KC, 2], F32)
        W = alloc_work(C, es)
        xv = xT_d.rearrange("(k p) t -> p k t", p=128)
        for si, (s0, n, o) in enumerate(SEGS):
            S.dma("sp", x[:, :, s0:s0 + n], xv[:, :, s0:s0 + n], (), x_tr[si])
        emit_mod(C, wada_d, bada_d, cc_d, norms_d, 0, modv, modv_tr)
        emit_ffn(C, W, x, x_tr, modv, modv_tr, 0, wg_d, wu_d, wd_d, 0)
        ov = out_d.rearrange("(k p) t -> p k t", p=128)
        out_tr = Tr("out")
        for si, (s0, n, o) in enumerate(SEGS):
            S.dma("sp", ov[:, :, s0:s0 + n], x[:, :, s0:s0 + n], (x_tr[si],), out_tr, nowaw=True)
        mo_tr = Tr("mo")
        S.dma("sp", modo_d, modv[:].rearrange("p a k o -> p (a k o)"), (modv_tr,), mo_tr)
        S.finish("sp", (out_tr, mo_tr))
        S.emit()
    return nc


NPROJ = 2816
DBG = {}


def emit_headnorm(C, H, src_ap, n, gain_ap, out_ap, out_tr, cos_ap=None, sin_ap=None, nowaw=False, nparts=128):
    S = C.S
    P = nparts
    hs, hs_tr = H["hs"]
    hsq, hsq_tr = H["hsq"]
    hr, hr_tr = H["hr"]
    hq, hq_tr = H["hq"]
    src_tr = H["src_tr"]
    S.op("act", lambda e: e.copy(hs[0:P, 0:n], src_ap), reads=(src_tr,), writes=(hs_tr,))
    S.op("dve", lambda e: e.tensor_tensor(hsq[0:P, 0:n], hs[0:P, 0:n], hs[0:P, 0:n], ALU.mult), reads=(hs_tr,), writes=(hsq_tr,))
    ps, ps_tr = C.ps()
    mm_group(C, ps, ps_tr, ps[0:P, 0:n], [(H["bd"][0][0:P, 0:P], hsq[0:P, 0:n])], (H["bd"][1], hsq_tr))
    S.op("dve", lambda e: e.tensor_scalar(hr[0:P, 0:n], ps[0:P, 0:n], 1.0 / 64, EPS, ALU.mult, ALU.add), reads=(ps_tr,), writes=(hr_tr,))
    S.op("act", lambda e: e.activation(hr[0:P, 0:n], hr[0:P, 0:n], AF.Sqrt), reads=(hr_tr,), writes=(hr_tr,))
    S.op("dve", lambda e: e.reciprocal(hr[0:P, 0:n], hr[0:P, 0:n]), reads=(hr_tr,), writes=(hr_tr,))
    if cos_ap is None:
        S.op("dve", lambda e: e.scalar_tensor_tensor(out_ap, hs[0:P, 0:n], gain_ap, hr[0:P, 0:n], ALU.mult, ALU.mult),
             reads=(hs_tr, hr_tr, H["gain_tr"]), writes=(out_tr,), nowaw=nowaw)
        return
    S.op("dve", lambda e: e.scalar_tensor_tensor(hq[0:P, 0:n], hs[0:P, 0:n], gain_ap, hr[0:P, 0:n], ALU.mult, ALU.mult),
         reads=(hs_tr, hr_tr, H["gain_tr"]), writes=(hq_tr,))
    ps3, ps3_tr = C.ps()
    mm_group(C, ps3, ps3_tr, ps3[0:P, 0:n], [(H["rmt"][0][0:P, 0:P], hq[0:P, 0:n])], (H["rmt"][1], hq_tr))
    S.op("dve", lambda e: e.tensor_tensor(hs[0:P, 0:n], hq[0:P, 0:n], cos_ap, ALU.mult), reads=(hq_tr, H["rope_tr"]), writes=(hs_tr,))
    S.op("dve", lambda e: e.tensor_tensor(hr[0:P, 0:n], ps3[0:P, 0:n], sin_ap, ALU.mult), reads=(ps3_tr, H["rope_tr"]), writes=(hr_tr,))
    S.op("dve", lambda e: e.tensor_tensor(out_ap, hs[0:P, 0:n], hr[0:P, 0:n], ALU.add), reads=(hs_tr, hr_tr), writes=(out_tr,), nowaw=nowaw)


def alloc_headnorm(C, es, cst_bd_d, cst_rmt_d):
    S = C.S
    H = {}
    H["hs"] = C.sb("h_hs", [128, 512], F32, es)
    H["hsq"] = C.sb("h_hsq", [128, 512], BF16, es)
    H["hr"] = C.sb("h_hr", [128, 512], F32, es)
    H["hq"] = C.sb("h_hq", [128, 512], F32, es)
    H["bd"] = C.sb("h_bd", [128, 128], BF16, es)
    H["rmt"] = C.sb("h_rmt", [128, 128], F32, es)
    S.dma("pool", H["bd"][0][:], cst_bd_d, (), H["bd"][1])
    S.dma("sp", H["rmt"][0][:], cst_rmt_d, (), H["rmt"][1])
    return H


def emit_inproj(C, W, x, x_tr, modv, modv_tr, win_d, l, proj_d, proj_tr, rope=None):
    S = C.S
    wv = win_d[l].rearrange("(k p) n -> p k n", p=128)
    pv = proj_d.rearrange("(j p) t -> p j t", p=128)
    with ExitStack() as es:
        xn, _ = C.sb("p_xn", [128, KC, NTOK], BF16, es)
        xn_tr = [Tr("pxn%d" % i) for i in range(len(SEGS))]
        wb = [C.sb("p_w%d" % i, [128, KC, 256], BF16, es) for i in range(2)]
        st = [C.sb("p_st%d" % i, [128, 512], F32, es) for i in range(4)]
        sti = 0
        for si, seg in enumerate(SEGS):
            emit_norm(C, W, x, x_tr, xn, xn_tr[si], seg[0], (si, seg), modv, modv_tr, 3)
        for jb in range(NPROJ // 256):
            w_t, w_tr = wb[jb % 2]
            S.dma("pool", w_t[:], wv[:, :, jb * 256:(jb + 1) * 256], (), w_tr)
            for si, (s0, n, o) in enumerate(SEGS):
                for jj in range(2):
                    j = jb * 2 + jj
                    ps, ps_tr = C.ps()
                    mm_group(C, ps, ps_tr, ps[:, 0:n],
                             [(w_t[:, k, jj * 128:(jj + 1) * 128], xn[:, k, s0:s0 + n]) for k in range(KC)],
                             (w_tr, xn_tr[si]))
                    s_t, s_tr = st[sti % 4]
                    eng = "act" if sti % 2 == 0 else "dve"
                    sti += 1
                    if rope is not None and j < 3:
                        H = rope["H"]
                        H["src_tr"] = ps_tr
                        H["gain_tr"] = rope["gain_tr"]
                        H["rope_tr"] = rope["tab_tr"]
                        gi = 0 if j < 2 else 1
                        emit_headnorm(C, H, ps[:, 0:n], n, rope["gain"][:, gi:gi + 1], s_t[:, 0:n], s_tr,
                                      cos_ap=rope["cos"][:, s0:s0 + n], sin_ap=rope["sin"][:, s0:s0 + n])
                    elif eng == "act":
                        S.op("act", lambda e, s_t=s_t, ps=ps, n=n: e.copy(s_t[:, 0:n], ps[:, 0:n]),
                             reads=(ps_tr,), writes=(s_tr,))
                    else:
                        S.op("dve", lambda e, s_t=s_t, ps=ps, n=n: e.tensor_copy(s_t[:, 0:n], ps[:, 0:n]),
                             reads=(ps_tr,), writes=(s_tr,))
                    S.dma("sp", pv[:, j, s0:s0 + n], s_t[:, 0:n], (s_tr,), proj_tr, nowaw=True)
        S.barrier()


def build_L1():
    nc = bass.Bass("TRN2", target_bir_lowering=False)
    with ExitStack() as es:
        C = Ctx(nc, es)
        S = C.S
        xT_d = C.dram_in("xT", [D, NTOK])
        cc_d = C.dram_in("cc", [128, KC, 2])
        wada_d = C.dram_in("w_ada", [1, D, 9 * D])
        bada_d = C.dram_in("b_ada", [1, 128, 72])
        norms_d = C.dram_in("norms", [1, 128, 3, KC])
        wg_d = C.dram_in("ffn1_wg", [1, D, DFF])
        wu_d = C.dram_in("ffn1_wu", [1, D, DFF])
        wd_d = C.dram_in("ffn1_wd", [1, DFF, D])
        win_d = C.dram_in("w_in", [1, D, INW])
        out_d = C.dram_out("xo", [D, NTOK])
        modo_d = C.dram_out("modo", [128, 9 * KC * 2])
        proj_d = C.dram_out("proj", [NPROJ, NTOK])
        cos_d = C.dram_in("rope_cos", [128, NTOK])
        sin_d = C.dram_in("rope_sin", [128, NTOK])
        gain_d = C.dram_in("a_gain", [128, 2])
        bd_d = C.dram_in("cst_bd", [128, 128])
        rmt_d = C.dram_in("cst_rmt", [128, 128])
        x, _ = C.sb("x", [128, KC, NTOK], F32)
        x_tr = [Tr("x%d" % i) for i in range(len(SEGS))]
        modv, modv_tr = C.sb("modv", [128, 9, KC, 2], F32)
        W = alloc_work(C, es)
        xv = xT_d.rearrange("(k p) t -> p k t", p=128)
        for si, (s0, n, o) in enumerate(SEGS):
            S.dma("sp", x[:, :, s0:s0 + n], xv[:, :, s0:s0 + n], (), x_tr[si])
        emit_mod(C, wada_d, bada_d, cc_d, norms_d, 0, modv, modv_tr)
        emit_ffn(C, W, x, x_tr, modv, modv_tr, 0, wg_d, wu_d, wd_d, 0)
        ov = out_d.rearrange("(k p) t -> p k t", p=128)
        out_tr = Tr("out")
        for si, (s0, n, o) in enumerate(SEGS):
            S.dma("sp", ov[:, :, s0:s0 + n], x[:, :, s0:s0 + n], (x_tr[si],), out_tr, nowaw=True)
        mo_tr = Tr("mo")
        S.dma("sp", modo_d, modv[:].rearrange("p a k o -> p (a k o)"), (modv_tr,), mo_tr)
        proj_tr = Tr("proj")
        with ExitStack() as es2:
            H = alloc_headnorm(C, es2, bd_d, rmt_d)
            cos, tab_tr = C.sb("r_cos", [128, NTOK], F32, es2)
            sin, _ = C.sb("r_sin", [128, NTOK], F32, es2)
            gain, gain_tr = C.sb("r_gain", [128, 2], F32, es2)
            S.dma("sp", cos[:], cos_d, (), tab_tr)
            S.dma("sp", sin[:], sin_d, (), tab_tr, nowaw=True)
            S.dma("sp", gain[:], gain_d, (), gain_tr)
            rope = dict(H=H, cos=cos, sin=sin, tab_tr=tab_tr, gain=gain, gain_tr=gain_tr)
            emit_inproj(C, W, x, x_tr, modv, modv_tr, win_d, 0, proj_d, proj_tr, rope=rope)
        S.finish("sp", (out_tr, mo_tr, proj_tr))
        S.emit()
    return nc


def host_small(inp, l):
    c = inp["c"][0]
    c_ctx = inp["c_ctx"]
    cc = np.ascontiguousarray(np.stack([c.reshape(8, 128).T, c_ctx.reshape(8, 128).T], axis=-1), np.float32)
    bada = np.ascontiguousarray(inp["b_ada"][l].reshape(72, 128).T[None])
    norms = np.ascontiguousarray(np.stack([inp["ffn1_norm"][l].reshape(8, 128).T, inp["mix_norm"][l].reshape(8, 128).T,
                                           inp["ffn2_norm"][l].reshape(8, 128).T], axis=1)[None])
    return cc, bada, norms


def run_L1(nc1, inp, l, xT_list, trace=False):
    cc, bada, norms = host_small(inp, l)
    in_maps = []
    for i in range(NCORES):
        in_maps.append(dict(xT=xT_list[i], cc=cc, w_ada=inp["w_ada"][l:l + 1], b_ada=bada, norms=norms,
                            ffn1_wg=inp["ffn1_w_gate"][l:l + 1], ffn1_wu=inp["ffn1_w_up"][l:l + 1],
                            ffn1_wd=inp["ffn1_w_down"][l:l + 1], w_in=inp["w_in"][l:l + 1]))
    res = run_bass_kernel_spmd(nc1, in_maps, core_ids=list(range(NCORES)), trace=trace)
    return res


NKEY = NCTX + SEQ
NKC = NKEY // 128
BIG = 30000.0
VW = 68
HALO_ROWS = 44
NHC = HALO_ROWS // 2
TB = NKEY
NCH = TB // 64
BLK = 26
NBLK = NCH // BLK


def emit_finalize_attn(C, acc, acc_tr, n, nheads, e64, o_sb, rden, stg, out_ap_fn, out_tr):
    S = C.S
    tot = nheads * n
    S.op("act", lambda e: e.copy(o_sb[0][0:65, 0:tot], acc[0:65, 0:tot]), reads=(acc_tr,), writes=(o_sb[1],))
    ps, ps_tr = C.ps()
    mm_group(C, ps, ps_tr, ps[0:64, 0:tot], [(e64[0][0:65, :], o_sb[0][0:65, 0:tot])], (e64[1], o_sb[1]))
    S.op("dve", lambda e: e.reciprocal(rden[0][0:64, 0:tot], ps[0:64, 0:tot]), reads=(ps_tr,), writes=(rden[1],))
    S.op("dve", lambda e: e.tensor_tensor(stg[0][0:64, 0:tot], o_sb[0][0:64, 0:tot], rden[0][0:64, 0:tot], ALU.mult),
         reads=(o_sb[1], rden[1]), writes=(stg[1],))
    for h in range(nheads):
        S.dma("sp", out_ap_fn(h), stg[0][0:64, h * n:(h + 1) * n], (stg[1],), out_tr, nowaw=True)


def emit_mixer_A(C, es0, qa_d, kt_d, va_d, ya_d, ya_tr, cst):
    S = C.S
    with ExitStack() as es:
        q, q_tr = C.sb("a_q", [64, 4, NTOK], BF16, es)
        kt, kt_tr = C.sb("a_kt", [64, 2, NKEY], BF16, es)
        v, v_tr = C.sb("a_v", [128, NKC, 2 * VW], BF16, es)
        pt = [C.sb("a_pt%d" % i, [128, 512], BF16, es) for i in range(3)]
        o_sb = C.sb("a_osb", [65, 512], F32, es)
        rden = C.sb("a_rden", [64, 512], F32, es)
        stg = C.sb("a_stg", [64, 512], F32, es)
        S.dma("pool", q[:], qa_d.rearrange("(h d) t -> d h t", d=64), (), q_tr)
        ktv = kt_d.rearrange("(h d) t -> d h t", d=64)
        for c4 in range(4):
            S.dma("pool", kt[:, :, c4 * 4160:(c4 + 1) * 4160], ktv[:, :, c4 * 4160:(c4 + 1) * 4160], (), kt_tr, nowaw=(c4 > 0))
        vv = va_d.rearrange("(c p) k e -> p c (k e)", p=128)
        for c2 in range(2):
            S.dma("pool", v[:, c2 * 65:(c2 + 1) * 65, :], vv[:, c2 * 65:(c2 + 1) * 65, :], (), v_tr, nowaw=(c2 > 0))
        acc, acc_tr = C.psum[7]
        pti = 0
        groups = [(s0, n, list(range(2, NKC)) + [0, 1]) for (s0, n, o) in SEGS[:4]] + [(NLAT, NCTX, [0, 1])]
        for (s0, n, chunks) in groups:
            for h in range(4):
                kv = h // 2
                g = h % 2
                pb = kv * 64
                for ci, kc in enumerate(chunks):
                    ps, ps_tr = C.ps()
                    mm_group(C, ps, ps_tr, ps[:, 0:n], [(kt[:, kv, kc * 128:(kc + 1) * 128], q[:, h, s0:s0 + n])],
                             (kt_tr, q_tr))
                    p_t, p_tr = pt[pti % 3]
                    pti += 1
                    S.op("act", lambda e, p_t=p_t, ps=ps, n=n: e.activation(p_t[:, 0:n], ps[:, 0:n], AF.Exp, scale=0.125),
                         reads=(ps_tr,), writes=(p_tr,))
                    first = ci == 0
                    last = ci == len(chunks) - 1
                    S.op("pe", lambda e, p_t=p_t, kc=kc, kv=kv, n=n, first=first, last=last: e.matmul(
                        acc[0:65, 0:n], v[:, kc, kv * VW:kv * VW + 65], p_t[:, 0:n], start=first, stop=last),
                        reads=(p_tr, v_tr), writes=(acc_tr,), nowaw=(not first))
                emit_finalize_attn(C, acc, acc_tr, n, 1, cst["e64"], o_sb, rden, stg,
                                   lambda hh, h=h, s0=s0, n=n: ya_d[h * 64:(h + 1) * 64, s0:s0 + n], ya_tr)
        S.barrier()


def emit_mixer_D(C, qd_d, kh_d, vh_d, kc_d, vc_d, bias_d, rvb_d, yd_d, yd_tr, cst):
    S = C.S
    with ExitStack() as es:
        q, q_tr = C.sb("d_q", [64, 4, NTOK], BF16, es)
        kh, kh_tr = C.sb("d_kh", [64, 4, HALO_ROWS * 64], BF16, es)
        vh, vh_tr = C.sb("d_vh", [128, NHC, 4 * VW], BF16, es)
        kc, kc_tr = C.sb("d_kc", [64, 4, NCTX], BF16, es)
        vc, vc_tr = C.sb("d_vc", [128, 2, 4 * VW], BF16, es)
        bias, bias_tr = C.sb("d_bias", [128, 7, 4, 128], F32, es)
        rvb, rvb_tr = C.sb("d_rvb", [128, 16, 7, 2], F32, es)
        tmp = [C.sb("d_tmp%d" % i, [128, 512], F32, es) for i in range(2)]
        pt = [C.sb("d_pt%d" % i, [128, 512], BF16, es) for i in range(3)]
        o_sb = C.sb("d_osb", [65, 512], F32, es)
        rden = C.sb("d_rden", [64, 512], F32, es)
        stg = C.sb("d_stg", [64, 512], F32, es)
        S.dma("pool", q[:], qd_d.rearrange("(h d) t -> d h t", d=64), (), q_tr)
        S.dma("pool", kh[:], kh_d.rearrange("(h d) t -> d h t", d=64), (), kh_tr)
        S.dma("pool", vh[:], vh_d.rearrange("(c p) h e -> p c (h e)", p=128), (), vh_tr)
        S.dma("pool", kc[:], kc_d.rearrange("(h d) t -> d h t", d=64), (), kc_tr)
        S.dma("pool", vc[:], vc_d.rearrange("(c p) h e -> p c (h e)", p=128), (), vc_tr)
        S.dma("sp", bias[:], bias_d, (), bias_tr)
        S.dma("sp", rvb[:], rvb_d, (), rvb_tr)
        acc, acc_tr = C.psum[7]
        ti = 0
        pti = 0
        groups = [(p * 128, 128, True, p) for p in range(16)] + [(NLAT, 128, False, 0), (NLAT + 128, 128, False, 0)]
        for (s0, n, local, p) in groups:
            klist = ([("h", j) for j in range(7)] if local else []) + [("c", 0), ("c", 1)]
            for ki, (kind, j) in enumerate(klist):
                ps, ps_tr = C.ps()

                def ksrc(h, kind=kind, j=j, p=p):
                    if kind == "h":
                        return kh[:, h, (p + j) * 128:(p + j + 1) * 128]
                    return kc[:, h, j * 128:(j + 1) * 128]

                def fn(e, ps=ps, ksrc=ksrc, s0=s0):
                    ins = None
                    for h in range(4):
                        ins = e.matmul(ps[:, h * 128:(h + 1) * 128], ksrc(h), q[:, h, s0:s0 + 128],
                                       start=True, stop=True)
                    return ins

                if DBG.get("D", 3) < 0.2:
                    continue
                S.op("pe", fn, reads=(kh_tr, kc_tr, q_tr), writes=(ps_tr,))
                p_t, p_tr = pt[pti % 3]
                pti += 1
                if DBG.get("D", 3) < 0.5:
                    continue
                if kind == "h":
                    t_t, t_tr = tmp[ti % 2]
                    ti += 1
                    S.op("dve", lambda e, t_t=t_t, ps=ps, j=j: e.scalar_tensor_tensor(
                        t_t[:, :], ps[:, :], 0.125, bias[:, j, :, :].rearrange("p h q -> p (h q)"), ALU.mult, ALU.add),
                        reads=(ps_tr, bias_tr), writes=(t_tr,))
                    for qr in range(2):
                        if DBG.get("D", 3) < 0.8:
                            continue
                        tv = t_t[:, :].rearrange("p (h r c) -> p h r c", h=4, r=2)[:, :, qr, :]
                        pv = p_t[:, :].rearrange("p (h r c) -> p h r c", h=4, r=2)[:, :, qr, :]
                        S.op("act", lambda e, tv=tv, pv=pv, p=p, j=j, qr=qr: e.activation(
                            pv, tv, AF.Exp, bias=rvb[:, p, j, qr:qr + 1]),
                            reads=(t_tr, rvb_tr), writes=(p_tr,), nowaw=(qr > 0))
                else:
                    S.op("act", lambda e, p_t=p_t, ps=ps: e.activation(p_t[:, :], ps[:, :], AF.Exp, scale=0.125),
                         reads=(ps_tr,), writes=(p_tr,))
                first = ki == 0
                last = ki == len(klist) - 1

                def fpv(e, p_t=p_t, kind=kind, j=j, p=p, first=first, last=last):
                    ins = None
                    for h in range(4):
                        vsrc = vh[:, p + j, h * VW:h * VW + 65] if kind == "h" else vc[:, j, h * VW:h * VW + 65]
                        ins = e.matmul(acc[0:65, h * 128:(h + 1) * 128], vsrc, p_t[:, h * 128:(h + 1) * 128],
                                       start=(first and h == 0), stop=(last and h == 3))
                    return ins

                if DBG.get("D", 3) >= 2:
                    S.op("pe", fpv, reads=(p_tr, vh_tr, vc_tr), writes=(acc_tr,), nowaw=(not first))
            if DBG.get("D", 3) >= 3:
                emit_finalize_attn(C, acc, acc_tr, 128, 4, cst["e64"], o_sb, rden, stg,
                                   lambda h, s0=s0: yd_d[h * 64:(h + 1) * 64, s0:s0 + 128], yd_tr)
        if DBG.get("D", 3) < 3:
            S.op("dve", lambda e: e.memset(stg[0][:], 0.0), writes=(stg[1],))
            S.dma("sp", yd_d[0:64, 0:512], stg[0][:], (stg[1],), yd_tr, nowaw=True)
        S.barrier()


def emit_mixer_C(C, xp_d, xpc_d, icl_d, icc_d, wbd_d, csc_d, yp_d, yp_tr):
    S = C.S
    with ExitStack() as es:
        wbd, wbd_tr = C.sb("c_wbd", [128, 2, 128], BF16, es)
        csc, csc_tr = C.sb("c_csc", [128, 2], F32, es)
        S.dma("pool", wbd[:], wbd_d.rearrange("c p n -> p c n"), (), wbd_tr)
        S.dma("sp", csc[:], csc_d, (), csc_tr)
        X, X_tr = C.sb("c_x", [128, 2064], F32, es)
        a, a_tr = C.sb("c_a", [128, 2064], F32, es)
        b, b_tr = C.sb("c_b", [128, 2064], F32, es)
        ic, ic_tr = C.sb("c_ic", [128, 2048], F32, es)
        pl, pl_tr = C.sb("c_pl", [128, 2048], BF16, es)
        stg = [C.sb("c_stg%d" % i, [128, 512], F32, es) for i in range(2)]
        sti = 0
        for (src_d, ic_d, n, t0) in ((xp_d, icl_d, NLAT, 0), (xpc_d, icc_d, NCTX, NLAT)):
            m = n + 16
            for c in range(2):
                S.dma("sp", X[:, 0:m], src_d[c * 128:(c + 1) * 128, :], (), X_tr)
                S.dma("sp", ic[:, 0:n], ic_d[c * 128:(c + 1) * 128, :], (), ic_tr)
                S.op("dve", lambda e, m=m: e.tensor_tensor(a[:, 1:m], X[:, 0:m - 1], X[:, 1:m], ALU.add),
                     reads=(X_tr,), writes=(a_tr,))
                S.op("dve", lambda e, m=m: e.tensor_tensor(b[:, 2:m - 1], a[:, 1:m - 2], a[:, 3:m], ALU.add),
                     reads=(a_tr,), writes=(b_tr,))
                if c == 0:
                    S.op("dve", lambda e, m=m: e.tensor_copy(b[0:64, 2:m - 1], a[0:64, 2:m - 1]),
                         reads=(a_tr, b_tr), writes=(b_tr,))
                    fin, fin_tr = b, b_tr
                else:
                    S.op("dve", lambda e, m=m: e.tensor_tensor(a[:, 4:m - 3], b[:, 2:m - 5], b[:, 6:m - 1], ALU.add),
                         reads=(b_tr,), writes=(a_tr,))
                    S.op("dve", lambda e, m=m: e.tensor_tensor(b[64:128, 8:m - 7], a[64:128, 4:m - 11], a[64:128, 12:m - 3],
                                                                ALU.add), reads=(a_tr,), writes=(b_tr,))
                    S.op("dve", lambda e, m=m: e.tensor_copy(b[0:64, 8:m - 8], a[0:64, 8:m - 8]),
                         reads=(a_tr, b_tr), writes=(b_tr,))
                    fin, fin_tr = b, b_tr
                S.op("dve", lambda e, n=n, fin=fin: e.tensor_tensor(a[:, 8:8 + n], fin[:, 8:8 + n], ic[:, 0:n], ALU.mult),
                     reads=(fin_tr, ic_tr), writes=(a_tr,))
                S.op("dve", lambda e, n=n: e.tensor_tensor(pl[:, 0:n], a[:, 8:8 + n], X[:, 8:8 + n], ALU.subtract),
                     reads=(a_tr, X_tr), writes=(pl_tr,))
                for g0 in range(0, n, 512):
                    gn = min(512, n - g0)
                    ps, ps_tr = C.ps()
                    mm_group(C, ps, ps_tr, ps[:, 0:gn], [(wbd[:, c, :], pl[:, g0:g0 + gn])], (wbd_tr, pl_tr))
                    s_t, s_tr = stg[sti % 2]
                    sti += 1
                    S.op("act", lambda e, s_t=s_t, ps=ps, gn=gn, c=c: e.activation(
                        s_t[:, 0:gn], ps[:, 0:gn], AF.Copy, scale=csc[:, c:c + 1]),
                        reads=(ps_tr, csc_tr), writes=(s_tr,))
                    S.dma("sp", yp_d[c * 128:(c + 1) * 128, t0 + g0:t0 + g0 + gn], s_t[:, 0:gn], (s_tr,), yp_tr, nowaw=True)
        S.barrier()


def emit_mixer_B(C, bq_d, bf_d, bv_d, lbl_d, lmsk_d, ob_d, ob_tr, cst):
    S = C.S
    n = BLK * 64
    with ExitStack() as es:
        SS, SS_tr = C.sb("b_ss", [64, NCH, 64], F32, es)
        dec, dec_tr = C.sb("b_dec", [64, NCH], F32, es)
        lb, lb_tr = C.sb("b_lb", [128, 8], F32, es)
        lw, lw_tr = C.sb("b_lw", [128, 12], F32, es)
        m01, m01_tr = C.sb("b_m01", [128, n], F32, es)
        fq, fq_tr = C.sb("b_q", [128, n], F32, es)
        ff, ff_tr = C.sb("b_f", [128, n], F32, es)
        lf, lf_tr = C.sb("b_lf", [128, n], F32, es)
        kk, kk_tr = C.sb("b_kk", [128, n], F32, es)
        cum, cum_tr = C.sb("b_cum", [128, n], F32, es)
        aa, aa_tr = C.sb("b_aa", [128, n], F32, es)
        ee, ee_tr = C.sb("b_ee", [128, n], F32, es)
        qt, qt_tr = C.sb("b_qt", [128, n], BF16, es)
        qtM, qtM_tr = C.sb("b_qtM", [128, n], BF16, es)
        ktM, ktM_tr = C.sb("b_ktM", [128, n], BF16, es)
        kt, kt_tr = C.sb("b_kt", [128, n], BF16, es)
        ktm, ktm_tr = C.sb("b_ktm", [64, n], BF16, es)
        vt, vt_tr = C.sb("b_v", [64, BLK, 64], BF16, es)
        sc3, sc3_tr = C.sb("b_sc3", [64, 3, BLK], F32, es)
        sb16, sb16_tr = C.sb("b_sb16", [64, BLK, 64], BF16, es)
        att, att_tr = C.sb("b_att", [64, 512], BF16, es)
        stg = [C.sb("b_stg%d" % i, [64, 512], F32, es) for i in range(2)]
        ident, ident_tr = cst["ident"]
        cmask, cmask_tr = cst["cmask"]
        bm, bm_tr = cst["bmask"]
        S.dma("sp", lw[0:64, 0:4], lbl_d, (), lw_tr)
        S.dma("sp", lw[64:128, 0:4], lbl_d, (), lw_tr, nowaw=True)
        S.dma("sp", lw[0:64, 4:8], lmsk_d, (), lw_tr, nowaw=True)
        S.dma("sp", lw[64:128, 4:8], lmsk_d, (), lw_tr, nowaw=True)
        S.op("act", lambda e: e.activation(lw[:, 8:12], lw[:, 0:4], AF.Exp), reads=(lw_tr,), writes=(lw_tr,))
        S.op("dve", lambda e: e.reduce_sum(lb[:, 0:1], lw[:, 8:12], mybir.AxisListType.X), reads=(lw_tr,), writes=(lb_tr,))
        S.op("dve", lambda e: e.tensor_tensor(lw[:, 0:4], lw[:, 8:12], lw[:, 4:8], ALU.mult), reads=(lw_tr,), writes=(lw_tr,))
        S.op("dve", lambda e: e.reduce_sum(lb[:, 1:2], lw[:, 0:4], mybir.AxisListType.X), reads=(lw_tr,), writes=(lb_tr,))
        S.op("dve", lambda e: e.reciprocal(lb[:, 2:3], lb[:, 0:1]), reads=(lb_tr,), writes=(lb_tr,))
        S.op("dve", lambda e: e.tensor_tensor(lb[:, 3:4], lb[:, 1:2], lb[:, 2:3], ALU.mult), reads=(lb_tr,), writes=(lb_tr,))
        S.op("dve", lambda e: e.tensor_scalar(lb[:, 4:5], lb[:, 3:4], -1.0, 1.0, ALU.mult, ALU.add), reads=(lb_tr,), writes=(lb_tr,))
        S.op("dve", lambda e: e.memset(m01[:], 1.0), writes=(m01_tr,))
        S.op("dve", lambda e: e.memset(m01[:].rearrange("p (c s) -> p c s", s=64)[:, :, 0:1], 0.0), reads=(m01_tr,), writes=(m01_tr,))

        def prep(blk, need_v):
            t0 = blk * n
            S.dma("sp", fq[0:64, :], bq_d[:, t0:t0 + n], (), fq_tr)
            S.dma("sp", fq[64:128, :], bq_d[:, t0:t0 + n], (), fq_tr, nowaw=True)
            S.dma("sp", ff[0:64, :], bf_d[:, t0:t0 + n], (), ff_tr)
            S.dma("sp", ff[64:128, :], bf_d[:, t0:t0 + n], (), ff_tr, nowaw=True)
            if need_v:
                S.dma("pool", vt[:], bv_d[:, blk * BLK:(blk + 1) * BLK, :], (), vt_tr)
            S.op("act", lambda e: e.activation(ff[:], ff[:], AF.Sigmoid), reads=(ff_tr,), writes=(ff_tr,))
            S.op("dve", lambda e: e.tensor_scalar(ff[:], ff[:], lb[:, 4:5], lb[:, 3:4], ALU.mult, ALU.add),
                 reads=(ff_tr, lb_tr), writes=(ff_tr,))
            S.op("act", lambda e: e.activation(lf[:], ff[:], AF.Ln), reads=(ff_tr,), writes=(lf_tr,))
            S.op("dve", lambda e: e.tensor_scalar(kk[:], ff[:], -1.0, 1.0, ALU.mult, ALU.add), reads=(ff_tr,), writes=(kk_tr,))
            S.op("dve", lambda e: e.tensor_tensor_scan(cum[:], m01[:], lf[:], 0.0, ALU.mult, ALU.add),
                 reads=(m01_tr, lf_tr), writes=(cum_tr,))
            c3 = cum[:].rearrange("p (c s) -> p c s", s=64)
            S.op("dve", lambda e: e.tensor_tensor(aa[:].rearrange("p (c s) -> p c s", s=64), c3,
                                                  c3[:, :, 31:32].to_broadcast([128, BLK, 64]), ALU.subtract),
                 reads=(cum_tr,), writes=(aa_tr,))
            S.op("act", lambda e: e.activation(ee[:], aa[:], AF.Exp), reads=(aa_tr,), writes=(ee_tr,))
            S.op("dve", lambda e: e.scalar_tensor_tensor(qt[:], fq[:], 0.125, ee[:], ALU.mult, ALU.mult),
                 reads=(fq_tr, ee_tr), writes=(qt_tr,))
            S.op("act", lambda e: e.activation(ee[:], aa[:], AF.Exp, scale=-1.0), reads=(aa_tr, qt_tr), writes=(ee_tr,))
            S.op("dve", lambda e: e.tensor_tensor(kt[:], kk[:], ee[:], ALU.mult), reads=(kk_tr, ee_tr), writes=(kt_tr,))
            S.op("dve", lambda e: e.tensor_tensor(qtM[:].rearrange("p (c s) -> p c s", s=64), qt[:].rearrange("p (c s) -> p c s", s=64),
                                                  bm[:, 0, :].unsqueeze(1).to_broadcast([128, BLK, 64]), ALU.mult),
                 reads=(qt_tr, bm_tr), writes=(qtM_tr,))
            S.op("dve", lambda e: e.tensor_tensor(ktM[:].rearrange("p (c s) -> p c s", s=64), kt[:].rearrange("p (c s) -> p c s", s=64),
                                                  bm[:, 1, :].unsqueeze(1).to_broadcast([128, BLK, 64]), ALU.mult),
                 reads=(kt_tr, bm_tr), writes=(ktM_tr,))
            S.op("act", lambda e: e.activation(sc3[:, 0, :], c3[0:64, :, 31], AF.Exp), reads=(cum_tr,), writes=(sc3_tr,))
            S.op("dve", lambda e: e.tensor_tensor(sc3[:, 2, :], c3[0:64, :, 63], c3[0:64, :, 31], ALU.subtract),
                 reads=(cum_tr, sc3_tr), writes=(sc3_tr,))
            S.op("act", lambda e: e.activation(sc3[:, 1, :], sc3[:, 2, :], AF.Exp), reads=(sc3_tr,), writes=(sc3_tr,))

        for blk in range(NBLK):
            prep(blk, True)
            S.op("act", lambda e, blk=blk: e.activation(dec[:, blk * BLK:(blk + 1) * BLK],
                                                        cum[0:64, :].rearrange("p (c s) -> p c s", s=64)[:, :, 63], AF.Exp),
                 reads=(cum_tr,), writes=(dec_tr,), nowaw=True)
            for g0 in range(0, BLK, 8):
                gn = min(8, BLK - g0)
                ps, ps_tr = C.ps()
                psb = ps[:].bitcast(BF16)

                def ftr(e, g0=g0, gn=gn, psb=psb):
                    ins = None
                    for i in range(gn):
                        ins = e.transpose(psb[0:64, i * 64:(i + 1) * 64], kt[0:64, (g0 + i) * 64:(g0 + i + 1) * 64], ident[0:64, 0:64])
                    return ins

                S.op("pe", ftr, reads=(kt_tr, ident_tr), writes=(ps_tr,))
                S.op("act", lambda e, g0=g0, gn=gn, psb=psb: e.copy(ktm[:, g0 * 64:(g0 + gn) * 64], psb[0:64, 0:gn * 64]),
                     reads=(ps_tr,), writes=(ktm_tr,), nowaw=(g0 > 0))
                ps2, ps2_tr = C.ps()

                def fkv(e, g0=g0, gn=gn, ps2=ps2):
                    ins = None
                    for i in range(gn):
                        ins = e.matmul(ps2[0:64, i * 64:(i + 1) * 64], ktm[:, (g0 + i) * 64:(g0 + i + 1) * 64], vt[:, g0 + i, :],
                                       start=True, stop=True)
                    return ins

                S.op("pe", fkv, reads=(ktm_tr, vt_tr), writes=(ps2_tr,))
                c0 = blk * BLK + g0
                S.op("dve", lambda e, g0=g0, gn=gn, ps2=ps2, c0=c0: e.tensor_tensor(
                    SS[:, c0:c0 + gn, :], ps2[0:64, 0:gn * 64].rearrange("p (c v) -> p c v", v=64),
                    sc3[:, 1, g0:g0 + gn].unsqueeze(2).to_broadcast([64, gn, 64]), ALU.mult),
                    reads=(ps2_tr, sc3_tr), writes=(SS_tr,), nowaw=True)
        for vv in range(64):
            S.op("dve", lambda e, vv=vv: e.tensor_tensor_scan(SS[:, :, vv], dec[:, :], SS[:, :, vv], 0.0, ALU.mult, ALU.add),
                 reads=(SS_tr, dec_tr), writes=(SS_tr,), nowaw=(vv > 0))
        sti = 0
        for blk in range(NBLK):
            prep(blk, True)
            lo = blk * BLK
            if blk == 0:
                S.op("dve", lambda e: e.memset(sb16[:, 0, :], 0.0), writes=(sb16_tr,))
                S.op("dve", lambda e: e.tensor_tensor(sb16[:, 1:BLK, :], SS[:, 0:BLK - 1, :],
                                                      sc3[:, 0, 1:BLK].unsqueeze(2).to_broadcast([64, BLK - 1, 64]), ALU.mult),
                     reads=(SS_tr, sc3_tr), writes=(sb16_tr,), nowaw=True)
            else:
                S.op("dve", lambda e, lo=lo: e.tensor_tensor(sb16[:, :, :], SS[:, lo - 1:lo - 1 + BLK, :],
                                                             sc3[:, 0, :].unsqueeze(2).to_broadcast([64, BLK, 64]), ALU.mult),
                     reads=(SS_tr, sc3_tr), writes=(sb16_tr,))
            for g0 in range(0, BLK, 8):
                gn = min(8, BLK - g0)
                ps, ps_tr = C.ps()

                def fatt(e, g0=g0, gn=gn, ps=ps):
                    ins = None
                    for i in range(gn):
                        sl = slice((g0 + i) * 64, (g0 + i + 1) * 64)
                        ins = e.matmul(ps[0:64, i * 64:(i + 1) * 64], ktM[:, sl], qtM[:, sl], start=True, stop=True)
                    return ins

                S.op("pe", fatt, reads=(ktM_tr, qtM_tr), writes=(ps_tr,))
                S.op("dve", lambda e, gn=gn, ps=ps: e.tensor_tensor(
                    att[:, 0:gn * 64].rearrange("p (c t) -> p c t", t=64), ps[0:64, 0:gn * 64].rearrange("p (c t) -> p c t", t=64),
                    cmask[0:64, 0:64].unsqueeze(1).to_broadcast([64, gn, 64]), ALU.mult),
                    reads=(ps_tr, cmask_tr), writes=(att_tr,))
                ps2, ps2_tr = C.ps()

                def fo(e, g0=g0, gn=gn, ps2=ps2):
                    ins = None
                    for i in range(gn):
                        sl = slice((g0 + i) * 64, (g0 + i + 1) * 64)
                        e.matmul(ps2[0:64, i * 64:(i + 1) * 64], vt[:, g0 + i, :], att[:, i * 64:(i + 1) * 64], start=True, stop=False)
                        ins = e.matmul(ps2[0:64, i * 64:(i + 1) * 64], sb16[:, g0 + i, :], qt[0:64, sl], start=False, stop=True)
                    return ins

                S.op("pe", fo, reads=(vt_tr, att_tr, sb16_tr, qt_tr), writes=(ps2_tr,))
                s_t, s_tr = stg[sti % 2]
                sti += 1
                S.op("act", lambda e, s_t=s_t, ps2=ps2, gn=gn: e.copy(s_t[:, 0:gn * 64], ps2[0:64, 0:gn * 64]),
                     reads=(ps2_tr,), writes=(s_tr,))
                t0 = (blk * BLK + g0) * 64
                S.dma("sp", ob_d[:, t0:t0 + gn * 64], s_t[:, 0:gn * 64], (s_tr,), ob_tr, nowaw=True)
        S.barrier()


def load_consts(C, es, need):
    S = C.S
    cst = {}
    if "e64" in need:
        d = C.dram_in("cst_e64", [65, 64])
        t = C.sb("k_e64", [65, 64], F32, es)
        S.dma("sp", t[0][:], d, (), t[1])
        cst["e64"] = t
    if "ident" in need:
        d = C.dram_in("cst_ident", [128, 128])
        t = C.sb("k_ident", [128, 128], BF16, es)
        S.dma("pool", t[0][:], d, (), t[1])
        cst["ident"] = t
    if "bmask" in need:
        d = C.dram_in("cst_bmask", [128, 2, 64])
        t = C.sb("k_bmask", [128, 2, 64], BF16, es)
        S.dma("pool", t[0][:], d, (), t[1])
        cst["bmask"] = t
    if "cmask" in need:
        d = C.dram_in("cst_cmask", [64, 64])
        t = C.sb("k_cmask", [64, 64], F32, es)
        S.dma("sp", t[0][:], d, (), t[1])
        cst["cmask"] = t
    return cst


def build_LB(parts=("A", "D", "C", "B")):
    nc = bass.Bass("TRN2", target_bir_lowering=False)
    with ExitStack() as es:
        C = Ctx(nc, es)
        C.psum = C.psum[:7] + [C.psum[7]]
        S = C.S
        C.ps = lambda: (C.__dict__.__setitem__("pi", C.pi + 1), C.psum[(C.pi - 1) % 7])[1]
        cst = load_consts(C, es, ("e64", "ident", "cmask", "bmask"))
        outs = []
        if "A" in parts:
            qa_d = C.dram_in("qa", [256, NTOK])
            kt_d = C.dram_in("kt", [128, NKEY])
            va_d = C.dram_in("va", [NKEY, 2, VW])
            ya_d = C.dram_out("ya", [256, NTOK])
            ya_tr = Tr("ya")
            emit_mixer_A(C, es, qa_d, kt_d, va_d, ya_d, ya_tr, cst)
            outs.append(ya_tr)
        if "D" in parts:
            qd_d = C.dram_in("qd", [256, NTOK])
            kh_d = C.dram_in("kh", [256, HALO_ROWS * 64])
            vh_d = C.dram_in("vh", [HALO_ROWS * 64, 4, VW])
            kc_d = C.dram_in("kc", [256, NCTX])
            vc_d = C.dram_in("vc", [NCTX, 4, VW])
            bias_d = C.dram_in("dbias", [128, 7, 4, 128])
            rvb_d = C.dram_in("rvb", [128, 16, 7, 2])
            yd_d = C.dram_out("yd", [256, NTOK])
            yd_tr = Tr("yd")
            emit_mixer_D(C, qd_d, kh_d, vh_d, kc_d, vc_d, bias_d, rvb_d, yd_d, yd_tr, cst)
            outs.append(yd_tr)
        if "C" in parts:
            xp_d = C.dram_in("xp", [256, NLAT + 16])
            xpc_d = C.dram_in("xpc", [256, NCTX + 16])
            icl_d = C.dram_in("icl", [256, NLAT])
            icc_d = C.dram_in("icc", [256, NCTX])
            wbd_d = C.dram_in("wbd", [2, 128, 128])
            csc_d = C.dram_in("csc", [128, 2])
            yp_d = C.dram_out("yp", [256, NTOK])
            yp_tr = Tr("yp")
            emit_mixer_C(C, xp_d, xpc_d, icl_d, icc_d, wbd_d, csc_d, yp_d, yp_tr)
            outs.append(yp_tr)
        if "B" in parts:
            bq_d = C.dram_in("bq", [64, TB])
            bf_d = C.dram_in("bf", [64, TB])
            bv_d = C.dram_in("bv", [64, NCH, 64])
            lbl_d = C.dram_in("lbl", [64, 4])
            lmsk_d = C.dram_in("lmsk", [64, 4])
            ob_d = C.dram_out("ob", [64, TB])
            ob_tr = Tr("ob")
            emit_mixer_B(C, bq_d, bf_d, bv_d, lbl_d, lmsk_d, ob_d, ob_tr, cst)
            outs.append(ob_tr)
        S.finish("sp", tuple(outs))
        S.emit()
    return nc


def host_consts():
    k = np.arange(128)
    bd = (k[:, None] // 64 == k[None, :] // 64).astype(np.float32)
    rmat = np.zeros((128, 128), np.float32)
    for m in range(128):
        dd = m % 64
        base = m - dd
        if dd % 32 < 16:
            rmat[m, base + dd + 16] = -1.0
        else:
            rmat[m, base + dd - 16] = 1.0
    rmt = np.ascontiguousarray(rmat.T)
    e64 = np.zeros((65, 64), np.float32)
    e64[64, :] = 1.0
    ident = np.eye(128, dtype=np.float32)
    s = np.arange(64)
    cmask = (s[:, None] <= s[None, :]).astype(np.float32)
    bmask = np.zeros((128, 2, 64), np.float32)
    bmask[0:64, 0, :] = 1.0
    bmask[64:128, 0, 32:] = 1.0
    bmask[0:64, 1, :32] = 1.0
    bmask[64:128, 1, 32:] = 1.0
    return dict(cst_bd=bd, cst_rmt=rmt, cst_e64=e64, cst_ident=ident, cst_cmask=cmask, cst_bmask=bmask)


def host_rope(i):
    half = 32
    inv = (1.0 / (np.float32(10000.0) ** (np.arange(0, half, 2, dtype=np.float32) / np.float32(half)))).astype(np.float32)
    t = np.arange(i * NLAT, (i + 1) * NLAT)
    row = (t // 64).astype(np.float32)
    col = (t % 64).astype(np.float32)
    cos = np.ones((128, NTOK), np.float32)
    sin = np.zeros((128, NTOK), np.float32)
    for p in range(128):
        dd = p % 64
        fi = dd % 16
        ang = (row if dd < 32 else col) * inv[fi]
        cos[p, :NLAT] = np.cos(ang.astype(np.float32))
        sin[p, :NLAT] = np.sin(ang.astype(np.float32))
    return cos, sin


def host_pool_invcount(T, lo, n):
    out = np.zeros((256, n), np.float32)
    pos = np.arange(lo, lo + n)
    for g, w in enumerate((2, 4, 8, 16)):
        a = np.clip(pos - w // 2, 0, T)
        b = np.clip(pos - w // 2 + w, 0, T)
        out[g * 64:(g + 1) * 64, :] = (1.0 / (b - a).astype(np.float32))[None, :]
    return out


def host_na_tables(rel_bias, i):
    kc = np.arange(64)
    qc = np.arange(64)
    cs = np.clip(qc - 8, 0, 48)
    colvalid = (kc[:, None] >= cs[None, :]) & (kc[:, None] < cs[None, :] + 16)
    dc = kc[:, None] - qc[None, :] + 15
    dcc = np.clip(dc, 0, 30)
    bias = np.full((128, 7, 4, 128), -BIG, np.float32)
    for j in range(7):
        for krl in range(2):
            for qrl in range(2):
                dr = 2 * j - 6 + krl - qrl
                if abs(dr) > 7:
                    continue
                for h in range(4):
                    row = rel_bias[h, dr + 7, :]
                    blk = np.where(colvalid, row[dcc], np.float32(-BIG))
                    bias[krl * 64:(krl + 1) * 64, j, h, qrl * 64:(qrl + 1) * 64] = blk
    rvb = np.full((128, 16, 7, 2), -BIG, np.float32)
    for p in range(16):
        for qrl in range(2):
            r = 32 * i + 2 * p + qrl
            rs = min(max(r - 4, 0), 248)
            for j in range(7):
                for krl in range(2):
                    kr = 32 * i + 2 * p - 6 + 2 * j + krl
                    if rs <= kr < rs + 8:
                        rvb[krl * 64:(krl + 1) * 64, p, j, qrl] = 0.0
    return bias, rvb


def prep_LB(inp, l, projs):
    P_lat = np.concatenate([p[:, :NLAT] for p in projs], axis=1)
    P_ctx = projs[0][:, NLAT:]
    cst = host_consts()
    kt = np.ascontiguousarray(np.concatenate([P_ctx[256:384], P_lat[256:384]], axis=1))
    va = np.concatenate([P_ctx[384:512].T, P_lat[384:512].T], axis=0).reshape(NKEY, 2, 64)
    va = np.ascontiguousarray(np.concatenate([va, np.ones((NKEY, 2, 1), np.float32), np.zeros((NKEY, 2, VW - 65), np.float32)], axis=2))
    kD = P_lat[2304:2560].reshape(256, 256, 64)
    vD = P_lat[2560:2816].T.reshape(256, 64, 4, 64)
    vD = np.concatenate([vD, np.ones((256, 64, 4, 1), np.float32), np.zeros((256, 64, 4, VW - 65), np.float32)], axis=3)
    kc = np.ascontiguousarray(P_ctx[2304:2560])
    vc = P_ctx[2560:2816].T.reshape(NCTX, 4, 64)
    vc = np.ascontiguousarray(np.concatenate([vc, np.ones((NCTX, 4, 1), np.float32), np.zeros((NCTX, 4, VW - 65), np.float32)], axis=2))
    xp = P_lat[1792:2048]
    xp_pad = np.concatenate([np.zeros((256, 8), np.float32), xp, np.zeros((256, 8), np.float32)], axis=1)
    xpc = np.ascontiguousarray(np.concatenate([np.zeros((256, 8), np.float32), P_ctx[1792:2048], np.zeros((256, 8), np.float32)], axis=1))
    icc = host_pool_invcount(NCTX, 0, NCTX)
    wg = inp["c_w_group"][l]
    wbd = np.zeros((2, 128, 128), np.float32)
    for c in range(2):
        wbd[c, 0:64, 0:64] = wg[2 * c]
        wbd[c, 64:128, 64:128] = wg[2 * c + 1]
    csc = np.ascontiguousarray(inp["c_scale"][l].reshape(2, 128).T)
    lmsk = np.zeros((64, 4), np.float32)
    lmsk[:, 1:l + 1] = 1.0
    in_maps = []
    for i in range(NCORES):
        m = dict(cst_e64=cst["cst_e64"], cst_ident=cst["cst_ident"], cst_cmask=cst["cst_cmask"], cst_bmask=cst["cst_bmask"])
        m["qa"] = np.ascontiguousarray(projs[i][0:256])
        m["kt"] = kt
        m["va"] = va
        m["qd"] = np.ascontiguousarray(projs[i][2048:2304])
        r0 = 32 * i - 6
        kh = np.zeros((256, HALO_ROWS, 64), np.float32)
        vh = np.zeros((HALO_ROWS, 64, 4, VW), np.float32)
        a = max(r0, 0)
        b = min(r0 + HALO_ROWS, 256)
        kh[:, a - r0:b - r0, :] = kD[:, a:b, :]
        vh[a - r0:b - r0] = vD[a:b]
        m["kh"] = kh.reshape(256, HALO_ROWS * 64)
        m["vh"] = vh.reshape(HALO_ROWS * 64, 4, VW)
        m["kc"] = kc
        m["vc"] = vc
        bias, rvb = host_na_tables(inp["d_rel_bias"][l], i)
        m["dbias"] = bias
        m["rvb"] = rvb
        m["xp"] = np.ascontiguousarray(xp_pad[:, i * NLAT:i * NLAT + NLAT + 16])
        m["xpc"] = xpc
        m["icl"] = host_pool_invcount(SEQ, i * NLAT, NLAT)
        m["icc"] = icc
        m["wbd"] = wbd
        m["csc"] = csc
        d, h = i // 4, i % 4
        q = np.concatenate([P_ctx[512 + h * 64:512 + (h + 1) * 64], P_lat[512 + h * 64:512 + (h + 1) * 64]], axis=1)
        f0 = 768 + d * 256 + h * 64
        f = np.concatenate([P_ctx[f0:f0 + 64], P_lat[f0:f0 + 64]], axis=1)
        v0 = 1280 + h * 64
        v = np.concatenate([P_ctx[v0:v0 + 64], P_lat[v0:v0 + 64]], axis=1)
        if d == 1:
            q = np.concatenate([q[:, :NCTX][:, ::-1], q[:, NCTX:][:, ::-1]], axis=1)
            f = np.concatenate([f[:, :NCTX][:, ::-1], f[:, NCTX:][:, ::-1]], axis=1)
            v = np.concatenate([v[:, :NCTX][:, ::-1], v[:, NCTX:][:, ::-1]], axis=1)
        m["bq"] = np.ascontiguousarray(q)
        m["bf"] = np.ascontiguousarray(f)
        m["bv"] = np.ascontiguousarray(v.T.reshape(NCH, 64, 64).transpose(1, 0, 2))
        m["lbl"] = np.ascontiguousarray(inp["b_lb_logits"][:, d, h * 64:(h + 1) * 64].T)
        m["lmsk"] = lmsk
        in_maps.append(m)
    return in_maps


def run_L1_full(nc1, inp, l, xT_list, trace=False):
    cc, bada, norms = host_small(inp, l)
    cst = host_consts()
    gain = np.ascontiguousarray(np.stack([np.tile(inp["a_q_norm"][l], 2), np.tile(inp["a_k_norm"][l], 2)], axis=1))
    in_maps = []
    for i in range(NCORES):
        cos, sin = host_rope(i)
        in_maps.append(dict(xT=xT_list[i], cc=cc, w_ada=inp["w_ada"][l:l + 1], b_ada=bada, norms=norms,
                            ffn1_wg=inp["ffn1_w_gate"][l:l + 1], ffn1_wu=inp["ffn1_w_up"][l:l + 1],
                            ffn1_wd=inp["ffn1_w_down"][l:l + 1], w_in=inp["w_in"][l:l + 1],
                            rope_cos=cos, rope_sin=sin, a_gain=gain, cst_bd=cst["cst_bd"], cst_rmt=cst["cst_rmt"]))
    return run_bass_kernel_spmd(nc1, in_maps, core_ids=list(range(NCORES)), trace=trace)


def emit_merge(C, W, x, x_tr, modv, modv_tr, ybr, ybr_tr, win_d, wbr_d, wout_d, l):
    S = C.S
    wgv = win_d[l].rearrange("(k p) n -> p k n", p=128)
    wbv = wbr_d[l].rearrange("b (k p) n -> p b k n", p=128)
    wov = wout_d[l].rearrange("(k p) n -> p k n", p=128)
    with ExitStack() as es:
        PT = max(sum(SEGS[si][1] for si in p) for p in FFN_PASSES)
        xn, _ = C.sb("m_xn", [128, KC, PT], BF16, es)
        mg, _ = C.sb("m_mg", [128, 2, PT], F32, es)
        mgb, _ = C.sb("m_mgb", [128, KC, PT], BF16, es)
        wgt = [C.sb("m_wg%d" % i, [128, KC, 256], BF16, es) for i in range(2)]
        wbt = [C.sb("m_wb%d" % i, [128, 2, 256], BF16, es) for i in range(2)]
        wot = [C.sb("m_wo%d" % i, [128, KC, 256], BF16, es) for i in range(2)]
        sg = [C.sb("m_sg%d" % i, [128, 512], F32, es) for i in range(2)]
        tp = [C.sb("m_tp%d" % i, [128, 512], F32, es) for i in range(2)]
        wi = 0
        ti = 0
        for p in FFN_PASSES:
            offs = {}
            o_ = 0
            xn_tr = {}
            mg_tr = {}
            mgb_tr = {}
            for si in p:
                offs[si] = o_
                o_ += SEGS[si][1]
                xn_tr[si] = Tr("mxn%d" % si)
                mg_tr[si] = Tr("mmg%d" % si)
                mgb_tr[si] = Tr("mmgb%d" % si)
                emit_norm(C, W, x, x_tr, xn, xn_tr[si], offs[si], (si, SEGS[si]), modv, modv_tr, 3)
            for cb in range(4):
                for nb in range(4):
                    g_t, g_tr = wgt[wi % 2]
                    b_t, b_tr = wbt[wi % 2]
                    wi += 1
                    c0 = NPROJ + nb * D + cb * 256
                    S.dma("pool", g_t[:], wgv[:, :, c0:c0 + 256], (), g_tr)
                    S.dma("pool", b_t[:], wbv[:, nb, :, cb * 256:(cb + 1) * 256], (), b_tr)
                    for si in p:
                        s0, n, o = SEGS[si]
                        xo = offs[si]
                        for cc_ in range(2):
                            pg, pg_tr = C.ps()
                            mm_group(C, pg, pg_tr, pg[:, 0:n],
                                     [(g_t[:, k, cc_ * 128:(cc_ + 1) * 128], xn[:, k, xo:xo + n]) for k in range(KC)],
                                     (g_tr, xn_tr[si]))
                            pu, pu_tr = C.ps()
                            mm_group(C, pu, pu_tr, pu[:, 0:n],
                                     [(b_t[:, kk, cc_ * 128:(cc_ + 1) * 128], ybr[:, nb, kk, s0:s0 + n]) for kk in range(2)],
                                     (b_tr, ybr_tr[si]))
                            st, st_tr = sg[ti % 2]
                            t_t, t_tr = tp[ti % 2]
                            ti += 1
                            S.op("act", lambda e, st=st, pg=pg, n=n: e.activation(st[:, 0:n], pg[:, 0:n], AF.Sigmoid),
                                 reads=(pg_tr,), writes=(st_tr,))
                            if nb == 0:
                                S.op("dve", lambda e, st=st, pu=pu, n=n, cc_=cc_, xo=xo: e.tensor_tensor(
                                    mg[:, cc_, xo:xo + n], st[:, 0:n], pu[:, 0:n], ALU.mult),
                                    reads=(st_tr, pu_tr), writes=(mg_tr[si],))
                            else:
                                S.op("dve", lambda e, st=st, pu=pu, n=n, t_t=t_t: e.tensor_tensor(
                                    t_t[:, 0:n], st[:, 0:n], pu[:, 0:n], ALU.mult),
                                    reads=(st_tr, pu_tr), writes=(t_tr,))
                                S.op("dve", lambda e, n=n, cc_=cc_, xo=xo, t_t=t_t: e.tensor_tensor(
                                    mg[:, cc_, xo:xo + n], mg[:, cc_, xo:xo + n], t_t[:, 0:n], ALU.add),
                                    reads=(t_tr, mg_tr[si]), writes=(mg_tr[si],))
                for si in p:
                    s0, n, o = SEGS[si]
                    xo = offs[si]
                    S.op("act", lambda e, cb=cb, xo=xo, n=n: e.copy(mgb[:, 2 * cb:2 * cb + 2, xo:xo + n], mg[:, :, xo:xo + n]),
                         reads=(mg_tr[si],), writes=(mgb_tr[si],), nowaw=(cb > 0))
            for cb in range(4):
                o_t, o_tr = wot[cb % 2]
                S.dma("pool", o_t[:], wov[:, :, cb * 256:(cb + 1) * 256], (), o_tr)
                for si in p:
                    s0, n, o = SEGS[si]
                    xo = offs[si]
                    for cc_ in range(2):
                        c = cb * 2 + cc_
                        py, py_tr = C.ps()
                        mm_group(C, py, py_tr, py[:, 0:n],
                                 [(o_t[:, k, cc_ * 128:(cc_ + 1) * 128], mgb[:, k, xo:xo + n]) for k in range(KC)],
                                 (o_tr, mgb_tr[si]))
                        S.op("dve", lambda e, py=py, n=n, c=c, s0=s0, o=o: e.scalar_tensor_tensor(
                            x[:, c, s0:s0 + n], py[:, 0:n], modv[:, 5, c, o:o + 1], x[:, c, s0:s0 + n], ALU.mult, ALU.add),
                            reads=(py_tr, modv_tr, x_tr[si]), writes=(x_tr[si],))
        S.barrier()


def build_L3():
    nc = bass.Bass("TRN2", target_bir_lowering=False)
    with ExitStack() as es:
        C = Ctx(nc, es)
        S = C.S
        xT_d = C.dram_in("xT", [D, NTOK])
        modv_d = C.dram_in("modv", [128, 9 * KC * 2])
        fn_d = C.dram_in("fnorm", [128, KC])
        ya_d = C.dram_in("ya", [256, NTOK])
        yp_d = C.dram_in("yp", [256, NTOK])
        yd_d = C.dram_in("yd", [256, NTOK])
        of_d = C.dram_in("of", [256, NTOK])
        ob_d = C.dram_in("obk", [256, NTOK])
        g_d = C.dram_in("bg", [256, NTOK])
        bog_d = C.dram_in("bo_gain", [128, 1])
        bd_d = C.dram_in("cst_bd", [128, 128])
        rmt_d = C.dram_in("cst_rmt", [128, 128])
        win_d = C.dram_in("w_in", [1, D, INW])
        wbr_d = C.dram_in("w_branch", [1, 4, 256, D])
        wout_d = C.dram_in("w_out", [1, D, D])
        wg_d = C.dram_in("ffn2_wg", [1, D, DFF])
        wu_d = C.dram_in("ffn2_wu", [1, D, DFF])
        wd_d = C.dram_in("ffn2_wd", [1, DFF, D])
        out_d = C.dram_out("xo", [D, NTOK])
        xf_d = C.dram_out("xf", [D, NLAT])
        x, _ = C.sb("x", [128, KC, NTOK], F32)
        x_tr = [Tr("x%d" % i) for i in range(len(SEGS))]
        modv, modv_tr = C.sb("modv", [128, 9, KC, 2], F32)
        fnv, fnv_tr = C.sb("fnv", [128, KC], F32)
        W = alloc_work(C, es)
        xv = xT_d.rearrange("(k p) t -> p k t", p=128)
        for si, (s0, n, o) in enumerate(SEGS):
            S.dma("sp", x[:, :, s0:s0 + n], xv[:, :, s0:s0 + n], (), x_tr[si])
        S.dma("sp", modv[:].rearrange("p a k o -> p (a k o)"), modv_d, (), modv_tr)
        S.dma("sp", fnv[:], fn_d, (), fnv_tr)
        with ExitStack() as es2:
            ybr, _ = C.sb("ybr", [128, 4, 2, NTOK], BF16, es2)
            ybr_tr = [Tr("ybr%d" % i) for i in range(len(SEGS))]
            for bi, src in ((0, ya_d), (2, yp_d), (3, yd_d)):
                sv = src.rearrange("(k p) t -> p k t", p=128)
                for si, (s0, n, o) in enumerate(SEGS):
                    S.dma("pool", ybr[:, bi, :, s0:s0 + n], sv[:, :, s0:s0 + n], (), ybr_tr[si], nowaw=True)
            with ExitStack() as es3:
                H = alloc_headnorm(C, es3, bd_d, rmt_d)
                bog, bog_tr = C.sb("bog", [128, 1], F32, es3)
                S.dma("sp", bog[:], bog_d, (), bog_tr)
                oa = [C.sb("b_oa%d" % i, [128, 512], F32, es3) for i in range(2)]
                ob_ = [C.sb("b_ob%d" % i, [128, 512], F32, es3) for i in range(2)]
                gg = [C.sb("b_gg%d" % i, [128, 512], F32, es3) for i in range(2)]
                nn, nn_tr = C.sb("b_nn", [128, 512], F32, es3)
                ofv = of_d.rearrange("(k p) t -> p k t", p=128)
                obv = ob_d.rearrange("(k p) t -> p k t", p=128)
                gv = g_d.rearrange("(k p) t -> p k t", p=128)
                it = 0
                for si, (s0, n, o) in enumerate(SEGS):
                    for c in range(2):
                        a_t, a_tr = oa[it % 2]
                        b_t, b_tr = ob_[it % 2]
                        g_t, g_tr = gg[it % 2]
                        it += 1
                        S.dma("sp", a_t[:, 0:n], ofv[:, c, s0:s0 + n], (), a_tr)
                        S.dma("sp", b_t[:, 0:n], obv[:, c, s0:s0 + n], (), b_tr)
                        S.dma("sp", g_t[:, 0:n], gv[:, c, s0:s0 + n], (), g_tr)
                        S.op("dve", lambda e, a_t=a_t, b_t=b_t, n=n: e.tensor_tensor(a_t[:, 0:n], a_t[:, 0:n], b_t[:, 0:n], ALU.add),
                             reads=(a_tr, b_tr), writes=(a_tr,))
                        S.op("act", lambda e, g_t=g_t, n=n: e.activation(g_t[:, 0:n], g_t[:, 0:n], AF.Silu), reads=(g_tr,), writes=(g_tr,))
                        H["src_tr"] = a_tr
                        H["gain_tr"] = bog_tr
                        emit_headnorm(C, H, a_t[:, 0:n], n, bog[:, 0:1], nn[:, 0:n], nn_tr)
                        S.op("dve", lambda e, g_t=g_t, n=n, c=c, s0=s0: e.tensor_tensor(ybr[:, 1, c, s0:s0 + n], nn[:, 0:n], g_t[:, 0:n], ALU.mult),
                             reads=(nn_tr, g_tr), writes=(ybr_tr[si],), nowaw=True)
            S.barrier()
            emit_merge(C, W, x, x_tr, modv, modv_tr, ybr, ybr_tr, win_d, wbr_d, wout_d, 0)
        emit_ffn(C, W, x, x_tr, modv, modv_tr, 6, wg_d, wu_d, wd_d, 0)
        ov = out_d.rearrange("(k p) t -> p k t", p=128)
        out_tr = Tr("out")
        for si, (s0, n, o) in enumerate(SEGS):
            S.dma("sp", ov[:, :, s0:s0 + n], x[:, :, s0:s0 + n], (x_tr[si],), out_tr, nowaw=True)
        xfv = xf_d.rearrange("(k p) t -> p k t", p=128)
        xf_tr = Tr("xf")
        sq, sq_tr = W["sq"]
        rs, rs_tr = W["rs"]
        tmp, tmp_tr = W["tmp"]
        for si, (s0, n, o) in enumerate(SEGS[:4]):
            for k in range(KC):
                S.op("dve", lambda e, k=k, s0=s0, n=n: e.tensor_tensor(sq[:, k, 0:n], x[:, k, s0:s0 + n], x[:, k, s0:s0 + n], ALU.mult),
                     reads=(x_tr[si],), writes=(sq_tr,), nowaw=(k > 0))
            ps, ps_tr = C.ps()
            mm_group(C, ps, ps_tr, ps[:, 0:n], [(W["ones"][0][:, :], sq[:, k, 0:n]) for k in range(KC)], (sq_tr, W["ones"][1]))
            S.op("dve", lambda e, ps=ps, n=n: e.tensor_scalar(rs[:, 0:n], ps[:, 0:n], 1.0 / D, EPS, ALU.mult, ALU.add),
                 reads=(ps_tr,), writes=(rs_tr,))
            S.op("act", lambda e, n=n: e.activation(rs[:, 0:n], rs[:, 0:n], AF.Sqrt), reads=(rs_tr,), writes=(rs_tr,))
            S.op("dve", lambda e, n=n: e.reciprocal(rs[:, 0:n], rs[:, 0:n]), reads=(rs_tr,), writes=(rs_tr,))
            for k in range(KC):
                t, t_tr = tmp[k % 2], tmp_tr[k % 2]
                S.op("dve", lambda e, k=k, t=t, s0=s0, n=n: e.scalar_tensor_tensor(
                    t[:, 0:n], x[:, k, s0:s0 + n], fnv[:, k:k + 1], rs[:, 0:n], ALU.mult, ALU.mult),
                    reads=(x_tr[si], rs_tr, fnv_tr), writes=(t_tr,))
                S.dma("sp", xfv[:, k, s0:s0 + n], t[:, 0:n], (t_tr,), xf_tr, nowaw=True)
        S.finish("sp", (out_tr, xf_tr))
        S.emit()
    return nc


def prep_L3(inp, l, x1_list, modo_list, lb_outs, projs):
    cst = host_consts()
    of = np.concatenate([lb_outs[h]["ob"] for h in range(4)], axis=0)
    obk = np.concatenate([lb_outs[4 + h]["ob"] for h in range(4)], axis=0)
    obk = np.concatenate([obk[:, :NCTX][:, ::-1], obk[:, NCTX:][:, ::-1]], axis=1)
    fn = np.ascontiguousarray(inp["final_norm"].reshape(8, 128).T)
    bog = np.ascontiguousarray(np.tile(inp["b_o_norm"][l], 2).reshape(128, 1))
    in_maps = []
    for i in range(NCORES):
        lat = slice(NCTX + i * NLAT, NCTX + (i + 1) * NLAT)
        m = dict(xT=x1_list[i], modv=modo_list[i], fnorm=fn,
                 ya=lb_outs[i]["ya"], yp=lb_outs[i]["yp"], yd=lb_outs[i]["yd"],
                 of=np.ascontiguousarray(np.concatenate([of[:, lat], of[:, :NCTX]], axis=1)),
                 obk=np.ascontiguousarray(np.concatenate([obk[:, lat], obk[:, :NCTX]], axis=1)),
                 bg=np.ascontiguousarray(projs[i][1536:1792]), bo_gain=bog,
                 cst_bd=cst["cst_bd"], cst_rmt=cst["cst_rmt"],
                 w_in=inp["w_in"][l:l + 1], w_branch=inp["w_branch"][l:l + 1], w_out=inp["w_out"][l:l + 1],
                 ffn2_wg=inp["ffn2_w_gate"][l:l + 1], ffn2_wu=inp["ffn2_w_up"][l:l + 1], ffn2_wd=inp["ffn2_w_down"][l:l + 1])
        in_maps.append(m)
    return in_maps


_PROGS = {}


def _prog(name, fn):
    if name not in _PROGS:
        _PROGS[name] = fn()
    return _PROGS[name]


def kernel(**inputs):
    inp = {k: np.asarray(v) for k, v in inputs.items()}
    x = inp["x"][0]
    ctx = inp["ctx"][0]
    xT = [np.ascontiguousarray(np.concatenate([x[i * NLAT:(i + 1) * NLAT].T, ctx.T], axis=1), dtype=np.float32)
          for i in range(NCORES)]
    nc1 = _prog("L1", build_L1)
    ncb = _prog("LB", build_LB)
    nc3 = _prog("L3", build_L3)
    cores = list(range(NCORES))
    res3 = None
    for l in range(4):
        r1 = run_L1_full(nc1, inp, l, xT).results
        projs = [r1[i]["proj"] for i in cores]
        rb = run_bass_kernel_spmd(ncb, prep_LB(inp, l, projs), core_ids=cores).results
        m3 = prep_L3(inp, l, [r1[i]["xo"] for i in cores], [r1[i]["modo"] for i in cores], rb, projs)
        res3 = run_bass_kernel_spmd(nc3, m3, core_ids=cores).results
        xT = [res3[i]["xo"] for i in cores]
    out = np.concatenate([res3[i]["xf"].T for i in cores], axis=0)[None]
    return np.ascontiguousarray(out, dtype=np.float32)
```
